# Optimizing a Trainium2 kernel written in Bass

```python
import jax, jax.numpy as jnp
from jax import lax
import numpy as np

D_MODEL = 2048
BATCH = 4
SEQ = 4096
DEPTH = 2

HEAD_DIM = 128
ROPE_THETA = 10000.0
GRID_W = 64
Q_BLOCK = 128
EPS = 1e-6
NEG = -1e30

A_HEADS = 4
A_Q_RANK = 512
A_KV_RANK = 512
A_NOPE = 128
A_ROPE = 64
A_V = 128
B_HEADS = 6
B_PATTERNS = ((128, 1), (512, 4), (2048, 16))
B_BLOCK = 64
C_HEADS = 6
C_KV_HEADS = 2
C_GROUP = C_HEADS // C_KV_HEADS

A_WIDTH = A_HEADS * A_V
B_WIDTH = B_HEADS * HEAD_DIM
C_WIDTH = C_HEADS * HEAD_DIM
MIX_WIDTH = A_WIDTH + B_WIDTH + C_WIDTH

IN_A = A_Q_RANK + A_KV_RANK + A_ROPE
IN_B = 3 * B_WIDTH
IN_C = C_WIDTH + 2 * C_KV_HEADS * HEAD_DIM
IN_WIDTH = IN_A + IN_B + IN_C

D_FF = -(-8 * D_MODEL // (3 * 256)) * 256

kernel_name = "hymba_mla_dilated_axial_gqa_encoder"


def rms_norm(x, g):
    xf = x.astype(jnp.float32)
    y = xf * lax.rsqrt(jnp.mean(xf * xf, axis=-1, keepdims=True) + EPS)
    return (y * g.astype(jnp.float32)).astype(x.dtype)


def rope_angles(pos, dim):
    inv = ROPE_THETA ** (-jnp.arange(0, dim, 2, dtype=jnp.float32) / dim)
    return pos.astype(jnp.float32)[:, None] * inv[None, :]


def apply_rope(x, ang):
    cos = jnp.cos(ang)[:, None, :]
    sin = jnp.sin(ang)[:, None, :]
    x1, x2 = jnp.split(x.astype(jnp.float32), 2, axis=-1)
    out = jnp.concatenate([x1 * cos - x2 * sin, x2 * cos + x1 * sin], axis=-1)
    return out.astype(x.dtype)


def dense_block_attention(q, k, v, scale):
    B, S, Hkv, G, Dk = q.shape
    nb = S // Q_BLOCK
    qb = jnp.moveaxis(q.reshape(B, nb, Q_BLOCK, Hkv, G, Dk), 1, 0)

    def attend(qblk):
        s = jnp.einsum('bqhgd,bkhd->bhgqk', qblk, k, preferred_element_type=jnp.float32) * scale
        p = jax.nn.softmax(s, axis=-1).astype(v.dtype)
        return jnp.einsum('bhgqk,bkhd->bqhgd', p, v)

    o = lax.map(attend, qb)
    return jnp.moveaxis(o, 0, 1).reshape(B, S, Hkv, G, v.shape[-1])


def dilated_pattern(q, k, v, window, dilation):
    B, S, H, D = q.shape
    half = window // (2 * dilation)
    L = S // dilation
    nb = -(-L // B_BLOCK)
    Lp = nb * B_BLOCK

    def to_res(t):
        return jnp.moveaxis(t.reshape(B, L, dilation, H, D), 2, 1)

    qr = jnp.pad(to_res(q), ((0, 0), (0, 0), (0, Lp - L), (0, 0), (0, 0)))
    kpad = ((0, 0), (0, 0), (B_BLOCK, Lp - L + B_BLOCK), (0, 0), (0, 0))
    kr = jnp.pad(to_res(k), kpad).reshape(B, dilation, nb + 2, B_BLOCK, H, D)
    vr = jnp.pad(to_res(v), kpad).reshape(B, dilation, nb + 2, B_BLOCK, H, D)
    qb = qr.reshape(B, dilation, nb, B_BLOCK, H, D)
    kb = jnp.concatenate([kr[:, :, :-2], kr[:, :, 1:-1], kr[:, :, 2:]], axis=3)
    vb = jnp.concatenate([vr[:, :, :-2], vr[:, :, 1:-1], vr[:, :, 2:]], axis=3)

    qi = jnp.arange(nb)[:, None] * B_BLOCK + jnp.arange(B_BLOCK)[None, :]
    kj = jnp.arange(nb)[:, None] * B_BLOCK - B_BLOCK + jnp.arange(3 * B_BLOCK)[None, :]
    rel = kj[:, None, :] - qi[:, :, None]
    mask = (jnp.abs(rel) <= half) & (kj[:, None, :] >= 0) & (kj[:, None, :] < L)

    s = jnp.einsum('brnqhd,brnkhd->brnhqk', qb, kb, preferred_element_type=jnp.float32) * (D ** -0.5)
    s = jnp.where(mask[:, None, :, :], s, NEG)
    m = jnp.max(s, axis=-1, keepdims=True)
    e = jnp.exp(s - m)
    den = jnp.sum(e, axis=-1)
    o = jnp.einsum('brnhqk,brnkhd->brnqhd', e.astype(v.dtype), vb, preferred_element_type=jnp.float32)
    o = o / jnp.moveaxis(den, -1, -2)[..., None]
    lse = jnp.moveaxis(m[..., 0] + jnp.log(den), -1, -2)

    def from_res(t):
        t = t.reshape((B, dilation, Lp) + t.shape[4:])[:, :, :L]
        return jnp.moveaxis(t, 1, 2).reshape((B, S) + t.shape[3:])

    return from_res(o), from_res(lse)


def mixer_mla(pa, q_norm, w_uq, kv_norm, w_ukv, ang_a):
    B, S, _ = pa.shape
    c_q, c_kv, k_rope = jnp.split(pa, [A_Q_RANK, A_Q_RANK + A_KV_RANK], axis=-1)
    q = (rms_norm(c_q, q_norm) @ w_uq).reshape(B, S, A_HEADS, A_NOPE + A_ROPE)
    q_nope, q_rope = jnp.split(q, [A_NOPE], axis=-1)
    q_rope = apply_rope(q_rope, ang_a)
    k_rope = apply_rope(k_rope[:, :, None, :], ang_a)
    kv = (rms_norm(c_kv, kv_norm) @ w_ukv).reshape(B, S, A_HEADS, A_NOPE + A_V)
    k_nope, v = jnp.split(kv, [A_NOPE], axis=-1)
    q_full = jnp.concatenate([q_nope, q_rope], axis=-1)[:, :, :, None, :]
    k_full = jnp.concatenate([k_nope, jnp.broadcast_to(k_rope, (B, S, A_HEADS, A_ROPE))], axis=-1)
    o = dense_block_attention(q_full, k_full, v, (A_NOPE + A_ROPE) ** -0.5)
    return o.reshape(B, S, A_WIDTH)


def mixer_dilated(pb, ang_1d):
    B, S, _ = pb.shape
    q, k, v = [t.reshape(B, S, B_HEADS, HEAD_DIM) for t in jnp.split(pb, 3, axis=-1)]
    q = apply_rope(q, ang_1d)
    k = apply_rope(k, ang_1d)
    outs, lses = [], []
    for window, dilation in B_PATTERNS:
        o, lse = dilated_pattern(q, k, v, window, dilation)
        outs.append(o)
        lses.append(lse)
    w = jax.nn.softmax(jnp.stack(lses, axis=0), axis=0)
    o = jnp.sum(w[..., None] * jnp.stack(outs, axis=0), axis=0)
    return o.astype(pb.dtype).reshape(B, S, B_WIDTH)


def mixer_axial_gqa(pc, q_norm, k_norm, ang_row, ang_col):
    B, S, _ = pc.shape
    q, k, v = jnp.split(pc, [C_WIDTH, C_WIDTH + C_KV_HEADS * HEAD_DIM], axis=-1)
    q = rms_norm(q.reshape(B, S, C_HEADS, HEAD_DIM), q_norm)
    k = rms_norm(k.reshape(B, S, C_KV_HEADS, HEAD_DIM), k_norm)
    v = v.reshape(B, S, C_KV_HEADS, HEAD_DIM)
    hd = HEAD_DIM // 2

    def axial(t):
        return jnp.concatenate([apply_rope(t[..., :hd], ang_row), apply_rope(t[..., hd:], ang_col)], axis=-1)

    q = axial(q).reshape(B, S, C_KV_HEADS, C_GROUP, HEAD_DIM)
    k = axial(k)
    o = dense_block_attention(q, k, v, HEAD_DIM ** -0.5)
    return o.reshape(B, S, C_WIDTH)


def setup_inputs(seed: int = 0) -> dict:
    key = jax.random.key(seed)
    ks = jax.random.split(key, 20)
    f32 = jnp.float32

    def nrm(k, shape, scale):
        return jax.random.normal(k, shape, f32) * scale

    def gain(k, shape):
        return 1.0 + 0.02 * jax.random.normal(k, shape, f32)

    return {
        "x": jax.random.normal(ks[0], (BATCH, SEQ, D_MODEL), f32),
        "attn_norm": gain(ks[1], (DEPTH, D_MODEL)),
        "w_in": nrm(ks[2], (DEPTH, D_MODEL, IN_WIDTH), D_MODEL ** -0.5),
        "a_q_norm": gain(ks[3], (DEPTH, A_Q_RANK)),
        "a_w_uq": nrm(ks[4], (DEPTH, A_Q_RANK, A_HEADS * (A_NOPE + A_ROPE)), A_Q_RANK ** -0.5),
        "a_kv_norm": gain(ks[5], (DEPTH, A_KV_RANK)),
        "a_w_ukv": nrm(ks[6], (DEPTH, A_KV_RANK, A_HEADS * (A_NOPE + A_V)), A_KV_RANK ** -0.5),
        "c_q_norm": gain(ks[7], (DEPTH, HEAD_DIM)),
        "c_k_norm": gain(ks[8], (DEPTH, HEAD_DIM)),
        "out_norm": gain(ks[9], (DEPTH, MIX_WIDTH)),
        "w_out": nrm(ks[10], (DEPTH, MIX_WIDTH, D_MODEL), MIX_WIDTH ** -0.5),
        "ffn_norm": gain(ks[11], (DEPTH, D_MODEL)),
        "w_gate": nrm(ks[12], (DEPTH, D_MODEL, D_FF), D_MODEL ** -0.5),
        "w_up": nrm(ks[13], (DEPTH, D_MODEL, D_FF), D_MODEL ** -0.5),
        "w_down": nrm(ks[14], (DEPTH, D_FF, D_MODEL), D_FF ** -0.5),
        "final_norm": gain(ks[15], (D_MODEL,)),
    }


def reference(x, attn_norm, w_in, a_q_norm, a_w_uq, a_kv_norm, a_w_ukv, c_q_norm, c_k_norm,
              out_norm, w_out, ffn_norm, w_gate, w_up, w_down, final_norm):
    B, S, _ = x.shape
    rows = S // GRID_W
    pos = jnp.arange(S, dtype=jnp.int32)
    row = jnp.repeat(jnp.arange(rows, dtype=jnp.int32), GRID_W)
    col = jnp.tile(jnp.arange(GRID_W, dtype=jnp.int32), rows)
    ang_1d = rope_angles(pos, HEAD_DIM)
    ang_a = rope_angles(pos, A_ROPE)
    ang_row = rope_angles(row, HEAD_DIM // 2)
    ang_col = rope_angles(col, HEAD_DIM // 2)

    for l in range(DEPTH):
        h = rms_norm(x, attn_norm[l])
        proj = jnp.einsum('bsd,de->bse', h, w_in[l])
        pa, pb, pc = jnp.split(proj, [IN_A, IN_A + IN_B], axis=-1)
        ya = mixer_mla(pa, a_q_norm[l], a_w_uq[l], a_kv_norm[l], a_w_ukv[l], ang_a)
        yb = mixer_dilated(pb, ang_1d)
        yc = mixer_axial_gqa(pc, c_q_norm[l], c_k_norm[l], ang_row, ang_col)
        g = out_norm[l]
        y = jnp.concatenate([
            rms_norm(ya, g[:A_WIDTH]),
            rms_norm(yb, g[A_WIDTH:A_WIDTH + B_WIDTH]),
            rms_norm(yc, g[A_WIDTH + B_WIDTH:]),
        ], axis=-1).astype(x.dtype)
        x = x + jnp.einsum('bse,ed->bsd', y, w_out[l])
        h = rms_norm(x, ffn_norm[l])
        ff = jax.nn.silu(h @ w_gate[l]) * (h @ w_up[l])
        x = x + ff @ w_down[l]
    return rms_norm(x, final_norm)
```

```python
import numpy as np
from contextlib import ExitStack
import ml_dtypes
import concourse.bass as bass
import concourse.mybir as mybir
from concourse.bass_utils import run_bass_kernel_spmd

F32 = mybir.dt.float32
BF16 = mybir.dt.bfloat16
AF = mybir.ActivationFunctionType
ALU = mybir.AluOpType

SEM_CHUNK = 30000
N_DMA_SLOTS = 6
DMA_SLOTS = {"pool": 16, "sp": 8}

D = 2048
S = 4096
TOK = 2048
DFF = 5632
INW = 4672
EPS = 1e-6
NQROWS = 4 * 192 + 768 + 768
NKROWS = 512 + 64 + 768 + 256
NVCOLS = 512 + 768 + 256
NMASK = 36
QA0, QB0, QC0 = 0, 768, 1536
KA0, KAR0, KB0, KC0 = 0, 512, 576, 1344
VA0, VB0, VC0 = 0, 512, 1280


class Tile:
    __slots__ = ("w", "r", "name", "excl")

    def __init__(self, name="", excl=False):
        self.w = None
        self.r = []
        self.name = name
        self.excl = excl


class Op:
    __slots__ = ("eng", "fn", "deps", "dma", "needed", "ev", "slot_prev", "cc")

    def __init__(self, eng, fn, dma):
        self.eng = eng
        self.fn = fn
        self.dma = dma
        self.cc = False
        self.deps = []
        self.needed = False
        self.ev = None
        self.slot_prev = None


class Prog:
    ENGS = ("pe", "act", "dve", "pool", "sp")

    def __init__(self, nc):
        self.nc = nc
        self.ops = {e: [] for e in self.ENGS}

    def op(self, eng, fn, reads=(), writes=(), dma=False):
        o = Op(eng, fn, dma)
        deps = []
        xr = [t for t in reads if t.excl]
        if xr:
            reads = [t for t in reads if not t.excl]
            writes = list(writes) + [t for t in xr if t not in writes]
        for t in reads:
            if t.w is not None:
                deps.append(t.w)
        for t in writes:
            if t.w is not None:
                deps.append(t.w)
            deps.extend(t.r)
        seen = set()
        for d in deps:
            if id(d) in seen:
                continue
            seen.add(id(d))
            if d.eng == "pe" and eng == "pe" and not d.dma and not dma:
                continue
            o.deps.append(d)
        for t in reads:
            t.r.append(o)
        for t in writes:
            t.w = o
            t.r = []
        self.ops[eng].append(o)
        return o

    def finalize(self, stack):
        nc = self.nc
        for e in self.ENGS:
            for o in self.ops[e]:
                for d in o.deps:
                    d.needed = True
        sems = {}

        def getsem(key):
            if key not in sems:
                sems[key] = stack.enter_context(nc.semaphore("s_%s_%s_%s" % key))
            return sems[key]

        ncc = [0]
        for e in self.ENGS:
            cnt = 0
            dcnt = 0
            nsl = DMA_SLOTS.get(e, N_DMA_SLOTS)
            slot_cnt = [0] * nsl
            slot_last = [None] * nsl
            for o in self.ops[e]:
                if o.cc:
                    ncc[0] += 1
                    o.ev = (getsem((e, "cc", ncc[0])), 1)
                elif o.dma:
                    s = dcnt % nsl
                    dcnt += 1
                    slot_cnt[s] += 1
                    assert slot_cnt[s] * 16 < 60000, "dma sem overflow"
                    o.slot_prev = slot_last[s]
                    o.ev = (getsem((e, "d", s)), slot_cnt[s] * 16)
                    slot_last[s] = o.ev
                elif o.needed:
                    cnt += 1
                    o.ev = (getsem((e, "c", (cnt - 1) // SEM_CHUNK)), (cnt - 1) % SEM_CHUNK + 1)
        block = stack.enter_context(nc.Block())
        ops = self.ops

        def emit(e, eng):
            known = {}
            for o in ops[e]:
                waits = [d.ev for d in o.deps]
                if o.dma and o.slot_prev is not None:
                    waits.append(o.slot_prev)
                for (sem, val) in waits:
                    k = id(sem)
                    if known.get(k, 0) >= val:
                        continue
                    eng.wait_ge(sem, val)
                    known[k] = val
                ins = o.fn(eng)
                if o.cc:
                    ins.then_inc(o.ev[0])
                elif o.dma:
                    ins.then_inc(o.ev[0], 16)
                elif o.needed:
                    ins.then_inc(o.ev[0], 1)

        @block.sync
        def _(eng):
            emit("sp", eng)

        @block.tensor
        def _(eng):
            emit("pe", eng)

        @block.scalar
        def _(eng):
            emit("act", eng)

        @block.vector
        def _(eng):
            emit("dve", eng)

        @block.gpsimd
        def _(eng):
            emit("pool", eng)


class Buf:
    def __init__(self, t, name, fence=None):
        self.t = t
        self.tile = Tile(name)
        self.tile.w = fence

    def __getitem__(self, k):
        return self.t[k]


class Bld:
    def __init__(self, nc, st):
        self.nc = nc
        self.st = st
        self.P = Prog(nc)
        self.n = 0
        self.fence = None
        self.scopes = []

    def push(self):
        self.scopes.append((self.st, []))
        self.st = ExitStack()

    def pop(self):
        outer, _ = self.scopes[-1]
        bufs = self.scopes[-1][1]
        self.fence = self.P.op("sp", lambda e: e.nop(), writes=[bf_.tile for bf_ in bufs])
        self.st.close()
        self.st = outer
        self.scopes.pop()

    def sb(self, shape, dt, name=None):
        self.n += 1
        name = ("%s_%d" % (name, self.n)) if name else "sb%d" % self.n
        bf_ = Buf(self.st.enter_context(self.nc.sbuf_tensor(name, list(shape), dt)), name, self.fence)
        if self.scopes:
            self.scopes[-1][1].append(bf_)
        return bf_

    def ps(self, shape, dt, name=None):
        self.n += 1
        name = name or "ps%d" % self.n
        bf_ = Buf(self.st.enter_context(self.nc.psum_tensor(name, list(shape), dt)), name)
        bf_.tile.excl = True
        return bf_

    def dma(self, q, out, in_, reads, writes, slow=False):
        if slow:
            self.P.op(q, lambda e: e.dma_start(out=out, in_=in_, allow_slow_non_contiguous=True),
                      reads=reads, writes=writes, dma=True)
        else:
            self.P.op(q, lambda e: e.dma_start(out=out, in_=in_), reads=reads, writes=writes, dma=True)

    def mm(self, out, lhsT, rhs, start, stop, reads, writes):
        self.P.op("pe", lambda e: e.matmul(out, lhsT=lhsT, rhs=rhs, start=start, stop=stop),
                  reads=reads, writes=writes)

    def tr(self, out, in_, ident, reads, writes):
        self.P.op("pe", lambda e: e.transpose(out, in_, ident), reads=reads, writes=writes)

    def act(self, out, in_, func, reads, writes, scale=None, bias=None, accum_out=None, eng="act"):
        kw = {}
        if scale is not None:
            kw["scale"] = scale
        if bias is not None:
            kw["bias"] = bias
        if accum_out is not None:
            kw["accum_out"] = accum_out
        self.P.op(eng, lambda e: e.activation(out=out, in_=in_, func=func, **kw), reads=reads, writes=writes)

    def ts(self, eng, out, in0, s1, s2, op0, op1, reads, writes):
        if op1 is None:
            self.P.op(eng, lambda e: e.tensor_scalar(out=out, in0=in0, scalar1=s1, scalar2=None, op0=op0),
                      reads=reads, writes=writes)
        else:
            self.P.op(eng, lambda e: e.tensor_scalar(out=out, in0=in0, scalar1=s1, scalar2=s2, op0=op0, op1=op1),
                      reads=reads, writes=writes)

    def tt(self, eng, out, in0, in1, op, reads, writes):
        self.P.op(eng, lambda e: e.tensor_tensor(out=out, in0=in0, in1=in1, op=op), reads=reads, writes=writes)

    def stt(self, eng, out, in0, scalar, in1, op0, op1, reads, writes):
        self.P.op(eng, lambda e: e.scalar_tensor_tensor(out=out, in0=in0, scalar=scalar, in1=in1, op0=op0, op1=op1),
                  reads=reads, writes=writes)

    def copy(self, eng, out, in_, reads, writes):
        if eng == "act":
            self.P.op("act", lambda e: e.copy(out=out, in_=in_), reads=reads, writes=writes)
        else:
            self.P.op(eng, lambda e: e.tensor_copy(out=out, in_=in_), reads=reads, writes=writes)

    def recip(self, out, in_, reads, writes):
        self.P.op("dve", lambda e: e.reciprocal(out=out, in_=in_), reads=reads, writes=writes)

    def memset(self, eng, ap, val, writes):
        self.P.op(eng, lambda e: e.memset(ap, val), writes=writes)

    def rstd(self, buf, ap, n):
        self.ts("dve", ap, ap, 1.0 / n, EPS, ALU.mult, ALU.add, [buf.tile], [buf.tile])
        self.act(ap, ap, AF.Sqrt, [buf.tile], [buf.tile])
        self.recip(ap, ap, [buf.tile], [buf.tile])


class Rot:
    def __init__(self, items):
        self.items = items
        self.i = 0

    def next(self):
        it = self.items[self.i % len(self.items)]
        self.i += 1
        return it


WIN_GROUPS = ((0, 512), (512, 512), (1024, 64), (1088, 512), (1600, 512), (2112, 512),
              (2624, 512), (3136, 256), (4416, 256), (3392, 512), (3904, 512))
WIN_GIDX = {c0: i for i, (c0, n) in enumerate(WIN_GROUPS)}

KCH = ((0, 512), (512, 64), (576, 512), (1088, 256), (1344, 256))


def kloc(row):
    for i, (r0, n) in enumerate(KCH):
        if r0 <= row < r0 + n:
            return i, row - r0
    raise ValueError(row)


def build_program():
    nc = bass.Bass("TRN2", target_bir_lowering=False)

    def din(name, shape, dt=F32):
        return nc.dram_tensor(name, list(shape), dt, kind="ExternalInput").ap()

    def dout(name, shape, dt=F32):
        return nc.dram_tensor(name, list(shape), dt, kind="ExternalOutput").ap()

    def dint(name, shape, dt=BF16):
        return nc.dram_tensor(name, list(shape), dt, kind="Internal").ap()

    st = ExitStack()
    with st:
        b = Bld(nc, st)
        P = b.P
        dts = {}

        def dtile(*key):
            if key not in dts:
                dts[key] = Tile(str(key))
            return dts[key]

        x_ext = din("x", [TOK, D])
        consts = din("rots", [3, 128, 128])
        cs = din("cs", [6, 128, TOK])
        bmask_d = din("bmask", [NMASK, 128, 512], BF16)
        final_norm_d = din("final_norm", [D])
        x_out = dout("x_out", [TOK, D])
        WSH = dict(attn_norm=[D], w_in=[D, INW], a_q_norm=[512], a_kv_norm=[512], a_w_uq=[512, 768],
                   a_w_ukv=[512, 1024], c_q_norm=[128], c_k_norm=[128], out_norm=[D], w_out=[D, D],
                   ffn_norm=[D], w_gate=[D, DFF], w_up=[D, DFF], w_down=[DFF, D])
        IO = []
        for l in range(2):
            d_ = {k: din("%s%d" % (k, l), shp) for k, shp in WSH.items()}
            d_["qt"] = dint("qt%d" % l, [NQROWS, TOK])
            d_["ksrc"] = [nc.dram_tensor("ks%d_%d" % (l, i), [n, TOK], BF16) for i, (r0, n) in enumerate(KCH)]
            d_["kdst"] = [nc.dram_tensor("kd%d_%d" % (l, i), [2 * n, TOK], BF16) for i, (r0, n) in enumerate(KCH)]
            d_["vsrc"] = [nc.dram_tensor("vs%d_%d" % (l, j), [512, NVCOLS], BF16) for j in range(4)]
            d_["vdst"] = [nc.dram_tensor("vd%d_%d" % (l, j), [1024, NVCOLS], BF16) for j in range(4)]
            d_["winb"] = dint("winb%d" % l, [len(WIN_GROUPS), 128, 16, 512])
            d_["woutb"] = dint("woutb%d" % l, [D, D])
            d_["wgb"] = dint("wgb%d" % l, [DFF // 256, 128, 16, 256])
            d_["wub"] = dint("wub%d" % l, [DFF // 256, 128, 16, 256])
            d_["wdb"] = dint("wdb%d" % l, [D // 256, 128, 44, 256])
            d_["L"] = l
            IO.append(d_)
        yT = dint("yT_s", [D, TOK], F32)
        x1s = dint("x1_s", [TOK, D], F32)
        x2s = dint("x2_s", [TOK, D], F32)
        io = IO[0]
        own_cache = {}

        def own_of(e):
            k = id(e)
            if k not in own_cache:
                own_cache[k] = e.snap(e.partition_id() % 2, min_val=0, max_val=1)
            return own_cache[k]

        par_cache = {}

        def par_of(e):
            k = id(e)
            if k not in par_cache:
                par_cache[k] = e.snap((e.partition_id() + 1) % 2, min_val=0, max_val=1)
            return par_cache[k]

        out_tiles = []

        identf = b.sb([128, 128], F32, "identf")
        ident = b.sb([128, 128], BF16, "ident")
        ones = b.sb([128, 128], BF16, "ones")
        b.memset("pool", identf[:], 1.0, [identf.tile])
        P.op("pool", lambda e: e.affine_select(out=identf[:], in_=identf[:], pattern=[[-1, 128]],
                                               compare_op=ALU.is_equal, fill=0.0, base=0, channel_multiplier=1),
             reads=[identf.tile], writes=[identf.tile])
        b.copy("dve", ident[:], identf[:], [identf.tile], [ident.tile])
        b.memset("dve", ones[:], 1.0, [ones.tile])
        rots = b.sb([128, 3, 128], BF16, "rots_sb")
        for r_ in range(3):
            b.dma("pool", rots[:, r_, :], consts[r_], [], [rots.tile])

        pb = [b.ps([128, 512], F32, "pb%d" % i) for i in range(8)]

        PCH = 1024
        pc_f = Rot([b.sb([128, PCH], F32) for _ in range(2)])
        pc_b = Rot([b.sb([128, PCH], BF16) for _ in range(2)])
        work = []
        wtiles = {}

        def add_precast(l, name, src, dst, ceng="pool"):
            R_, C_ = src.shape
            n = R_ * C_ // 128
            assert n % PCH == 0
            sv = src.rearrange("(p a) c -> p (a c)", p=128)
            dv = dst.rearrange("(p a) c -> p (a c)", p=128)
            tl = []
            for i in range(n // PCH):
                t = dtile("wc", l, name, i)
                tl.append(t)

                def step(i=i, t=t):
                    f = pc_f.next()
                    g = pc_b.next()
                    b.dma("sp", f[:], sv[:, i * PCH:(i + 1) * PCH], [], [f.tile])
                    b.copy(ceng, g[:], f[:], [f.tile], [g.tile])
                    b.dma("pool", dv[:, i * PCH:(i + 1) * PCH], g[:], [g.tile], [t])
                work.append(((l, name), step))
            wtiles[(l, name)] = tl

        def add_precast_blocked(l, name, srcv, dstb, ngroups, ninner):
            tl = []
            for g_ in range(ngroups):
                for i4 in range(ninner // 4):
                    t = dtile("wc", l, name, g_, i4)
                    tl.append(t)

                    def step(g_=g_, i4=i4, t=t):
                        f = pc_f.next()
                        g = pc_b.next()
                        b.dma("sp", f[:].rearrange("p (a c) -> p a c", a=4),
                              srcv[:, i4 * 4:(i4 + 1) * 4, g_ * 256:(g_ + 1) * 256], [], [f.tile])
                        b.copy("pool", g[:], f[:], [f.tile], [g.tile])
                        b.dma("pool", dstb[g_, :, i4 * 4:(i4 + 1) * 4, :], g[:].rearrange("p (a c) -> p a c", a=4), [g.tile], [t])
                    work.append(((l, name), step))
            wtiles[(l, name)] = tl

        def add_precast_win(l, src, dstb, ceng):
            srcv = src.rearrange("(k p) n -> p k n", p=128)
            tl = {}
            for g_, (c0, ncol) in enumerate(WIN_GROUPS):
                rpc = PCH // ncol
                tl[g_] = []
                for j in range(16 // rpc):
                    t = dtile("wc", l, "in", g_, j)
                    tl[g_].append(t)

                    def step(g_=g_, j=j, t=t, c0=c0, ncol=ncol, rpc=rpc):
                        f = pc_f.next()
                        g = pc_b.next()
                        b.dma("sp", f[:].rearrange("p (a c) -> p a c", a=rpc), srcv[:, j * rpc:(j + 1) * rpc, c0:c0 + ncol], [], [f.tile])
                        b.copy(ceng, g[:], f[:], [f.tile], [g.tile])
                        b.dma("pool", dstb[g_, :, j * rpc:(j + 1) * rpc, 0:ncol], g[:].rearrange("p (a c) -> p a c", a=rpc), [g.tile], [t])
                    work.append(((l, "in", g_), step))
            wtiles[(l, "in")] = tl

        def drain_tag(tag):
            while any(t_ == tag for t_, _ in work):
                work.pop(0)[1]()

        def pump(n):
            for _ in range(n):
                if work:
                    work.pop(0)[1]()

        def drain(l, name):
            while any(tag[:2] == (l, name) for tag, _ in work):
                work.pop(0)[1]()

        for l_ in range(2):
            add_precast_win(l_, IO[l_]["w_in"], IO[l_]["winb"], "dve" if l_ == 0 else "pool")
            add_precast(l_, "out", IO[l_]["w_out"], IO[l_]["woutb"])
            add_precast_blocked(l_, "g", IO[l_]["w_gate"].rearrange("(k p) n -> p k n", p=128), IO[l_]["wgb"], DFF // 256, 16)
            add_precast_blocked(l_, "u", IO[l_]["w_up"].rearrange("(k p) n -> p k n", p=128), IO[l_]["wub"], DFF // 256, 16)
            add_precast_blocked(l_, "d", IO[l_]["w_down"].rearrange("(f p) n -> p f n", p=128), IO[l_]["wdb"], D // 256, 44)

        def nt_alloc(nx):
            return dict(gb=b.sb([128, D], F32), xts=Rot([b.sb([128, D], F32) for _ in range(nx)]),
                        xn=Rot([b.sb([128, D], BF16) for _ in range(2)]),
                        sss=Rot([b.sb([128, 1], F32) for _ in range(2)]), tps=Rot([pb[6], pb[7]]))

        def norm_transpose(x_src, row0, ntile, deps, g_dram, hT, tmp, keep=None, per_tile=None):
            gb = tmp["gb"]
            b.dma("sp", gb[:], g_dram.partition_broadcast(128), [], [gb.tile])
            for i in range(ntile):
                xt = keep[i] if keep is not None else tmp["xts"].next()
                xn = tmp["xn"].next()
                ss = tmp["sss"].next()
                r0 = row0 + i * 128
                b.dma("sp", xt[:], x_src[r0:r0 + 128, :], [deps(r0 // 128)] if deps else [], [xt.tile])
                b.act(xn[:], xt[:], AF.Square, [xt.tile], [xn.tile, ss.tile], accum_out=ss[:, 0:1])
                b.rstd(ss, ss[:, 0:1], D)
                b.stt("dve", xn[:], xt[:], ss[:, 0:1], gb[:], ALU.mult, ALU.mult, [xt.tile, ss.tile, gb.tile], [xn.tile])
                for half in range(2):
                    tp = tmp["tps"].next()
                    tpv = tp[:].bitcast(BF16)
                    for j in range(8):
                        k = half * 8 + j
                        b.tr(tpv[:, j * 128:(j + 1) * 128], xn[:, k * 128:(k + 1) * 128], ident[:],
                             [xn.tile, ident.tile], [tp.tile])
                    b.copy("dve", hT[:, half * 8:(half + 1) * 8, i * 128:(i + 1) * 128],
                           tpv.rearrange("p (k t) -> p k t", k=8), [tp.tile], [hT.tile])
                if per_tile is not None:
                    per_tile()

        def phase1(x_src, xdeps):
            b.push()
            w_in = io["w_in"]
            hTs = [b.sb([128, 16, 512], BF16, "hT") for _ in range(4)]
            tmp = dict(gb=b.sb([128, D], F32), xts=Rot([b.sb([128, D], F32) for _ in range(1)]),
                       xn=Rot([b.sb([128, D], BF16) for _ in range(1)]),
                       sss=Rot([b.sb([128, 1], F32) for _ in range(2)]), tps=Rot([pb[6], pb[7]]))
            norm_transpose(x_src, 0, 4, xdeps, io["attn_norm"], hTs[0], tmp)
            wuq = b.sb([128, 4, 768], BF16, "wuq")
            wukv = b.sb([128, 4, 1024], BF16, "wukv")
            for k in range(4):
                b.dma("pool", wuq[:, k, :], io["a_w_uq"][k * 128:(k + 1) * 128, :], [], [wuq.tile])
                b.dma("pool", wukv[:, k, :], io["a_w_ukv"][k * 128:(k + 1) * 128, :], [], [wukv.tile])
            gains = b.sb([128, 10], F32, "gains")
            b.dma("sp", gains[:, 0:4], io["a_q_norm"].rearrange("(k p) -> p k", p=128), [], [gains.tile], slow=True)
            b.dma("sp", gains[:, 4:8], io["a_kv_norm"].rearrange("(k p) -> p k", p=128), [], [gains.tile], slow=True)
            b.dma("sp", gains[:, 8:9], io["c_q_norm"].rearrange("(k p) -> p k", p=128), [], [gains.tile], slow=True)
            b.dma("sp", gains[:, 9:10], io["c_k_norm"].rearrange("(k p) -> p k", p=128), [], [gains.tile], slow=True)
            winb = io["winb"]
            win_t = wtiles[(io["L"], "in")]
            LLw = io["L"]
            wbufs = Rot([b.sb([128, 16, 512], BF16) for _ in range(2)])
            csts = Rot([b.sb([128, 6, 512], F32) for _ in range(1)])
            accs = Rot([pb[0], pb[1], pb[2]])
            aux = Rot([pb[3], pb[4]])
            stg_f = Rot([b.sb([128, 512], F32) for _ in range(3)])
            stg_b = Rot([b.sb([128, 512], BF16) for _ in range(3)])
            obf = Rot([b.sb([128, 512], BF16) for _ in range(4)])
            sqb = Rot([b.sb([128, 512], BF16) for _ in range(2)])
            rsb = Rot([b.sb([128, 512], F32) for _ in range(2)])
            t1b = Rot([b.sb([128, 512], F32) for _ in range(4)])
            cq = b.sb([128, 4, 512], F32, "cq")
            cqn = b.sb([128, 4, 512], BF16, "cqn")
            qt_o = io["qt"]
            LL = io["L"]

            def kdst_ap(row, tsl_):
                i_, lr = kloc(row)
                return io["ksrc"][i_][lr:lr + 128, tsl_] if True else None

            def otile(kind, *key):
                return dtile("o", LL, kind, *key)

            def load_w(c0, ncol):
                wb = wbufs.next()
                gi = WIN_GIDX[c0]
                assert WIN_GROUPS[gi][1] == ncol
                drain_tag((LLw, "in", gi))
                if gi + 1 < len(WIN_GROUPS):
                    drain_tag((LLw, "in", gi + 1))
                for k8 in range(2):
                    b.dma("sp", wb[:, k8 * 8:(k8 + 1) * 8, 0:ncol], winb[gi, :, k8 * 8:(k8 + 1) * 8, 0:ncol], win_t[gi], [wb.tile])
                pump(1)
                return wb

            def fmm(acc, wb, off, ncol, sbk):
                for k in range(16):
                    b.mm(acc[0:ncol, :], wb[:, k, off:off + ncol], hTs[sbk][:, k, :],
                         k == 0, k == 15, [wb.tile, hTs[sbk].tile], [acc.tile])

            def rope_store(src_f, src_tile, srcb, npart, ridx, tab, dst_ap, dst_tile, cst):
                rp = aux.next()
                b.mm(rp[0:npart, :], rots[0:npart, ridx, 0:npart], srcb[0:npart, :], True, True,
                     [rots.tile, srcb.tile], [rp.tile])
                t1 = t1b.next()
                ob = obf.next()
                b.tt("dve", t1[0:npart, :], src_f, cst[0:npart, 2 * tab, :], ALU.mult, [src_tile, cst.tile], [t1.tile])
                t2 = t1b.next()
                b.tt("dve", t2[0:npart, :], rp[0:npart, :], cst[0:npart, 2 * tab + 1, :], ALU.mult, [rp.tile, cst.tile], [t2.tile])
                b.tt("dve", ob[0:npart, :], t1[0:npart, :], t2[0:npart, :], ALU.add, [t1.tile, t2.tile], [ob.tile])
                b.dma("pool", dst_ap, ob[0:npart, :], [ob.tile], [dst_tile])

            for sbk in range(4):
                tsl = slice(sbk * 512, (sbk + 1) * 512)
                cst = csts.next()
                for r_ in range(6):
                    b.dma("sp", cst[:, r_, :], cs[r_, :, tsl], [], [cst.tile])
                for which in range(2):
                    wb = load_w(which * 512, 512)
                    for c in range(4):
                        acc = accs.next()
                        fmm(acc, wb, c * 128, 128, sbk)
                        b.copy("dve", cq[:, c, :], acc[:], [acc.tile], [cq.tile])
                    sp_ = aux.next()
                    for c in range(4):
                        sq = sqb.next()
                        b.act(sq[:], cq[:, c, :], AF.Square, [cq.tile], [sq.tile])
                        b.mm(sp_[:], ones[:], sq[:], c == 0, c == 3, [ones.tile, sq.tile], [sp_.tile])
                    rs = rsb.next()
                    b.copy("dve", rs[:], sp_[:], [sp_.tile], [rs.tile])
                    b.rstd(rs, rs[:], 512)
                    for c in range(4):
                        b.stt("dve", cqn[:, c, :], cq[:, c, :], gains[:, which * 4 + c:which * 4 + c + 1], rs[:],
                              ALU.mult, ALU.mult, [cq.tile, gains.tile, rs.tile], [cqn.tile])
                    if which == 0:
                        for h in range(4):
                            acc = accs.next()
                            for k in range(4):
                                b.mm(acc[:], wuq[:, k, h * 192:h * 192 + 128], cqn[:, k, :], k == 0, k == 3,
                                     [wuq.tile, cqn.tile], [acc.tile])
                            ob = obf.next()
                            b.copy("dve", ob[:], acc[:], [acc.tile], [ob.tile])
                            b.dma("pool", qt_o[QA0 + h * 192:QA0 + h * 192 + 128, tsl], ob[:], [ob.tile], [otile("q", "a", h, sbk)])
                            acc = accs.next()
                            for k in range(4):
                                b.mm(acc[0:64, :], wuq[:, k, h * 192 + 128:h * 192 + 192], cqn[:, k, :], k == 0, k == 3,
                                     [wuq.tile, cqn.tile], [acc.tile])
                            sbf = stg_b.next()
                            b.copy("dve", sbf[0:64, :], acc[0:64, :], [acc.tile], [sbf.tile])
                            rope_store(acc[0:64, :], acc.tile, sbf, 64, 0, 0,
                                       qt_o[QA0 + h * 192 + 128:QA0 + h * 192 + 192, tsl], otile("q", "ar", h, sbk), cst)
                    else:
                        for h in range(4):
                            acc = accs.next()
                            for k in range(4):
                                b.mm(acc[:], wukv[:, k, h * 256:h * 256 + 128], cqn[:, k, :], k == 0, k == 3,
                                     [wukv.tile, cqn.tile], [acc.tile])
                            ob = obf.next()
                            b.copy("dve", ob[:], acc[:], [acc.tile], [ob.tile])
                            b.dma("pool", io["ksrc"][0][h * 128:(h + 1) * 128, tsl], ob[:], [ob.tile], [otile("k", 0, h, sbk)])
                        wv4 = wukv[:, :, :].rearrange("p k (h t d) -> p k h t d", h=4, t=2)
                        for tt_ in range(4):
                            acc = accs.next()
                            for k in range(4):
                                b.mm(acc[:].rearrange("p (h d) -> p h d", h=4), cqn[:, k, tt_ * 128:(tt_ + 1) * 128],
                                     wv4[:, k, :, 1, :], k == 0, k == 3, [wukv.tile, cqn.tile], [acc.tile])
                            ob = obf.next()
                            b.copy("dve", ob[:], acc[:], [acc.tile], [ob.tile])
                            r0 = sbk * 512 + tt_ * 128
                            b.dma("pool", io["vsrc"][sbk][tt_ * 128:(tt_ + 1) * 128, VA0:VA0 + 512], ob[:], [ob.tile], [otile("v", sbk, "a", tt_)])
                wb = load_w(1024, 64)
                acc = accs.next()
                fmm(acc, wb, 0, 64, sbk)
                sbf = stg_b.next()
                b.copy("dve", sbf[0:64, :], acc[0:64, :], [acc.tile], [sbf.tile])
                rope_store(acc[0:64, :], acc.tile, sbf, 64, 0, 0, io["ksrc"][1][0:64, tsl], otile("k", 1, 0, sbk), cst)
                for g in range(3):
                    wb = load_w(1088 + g * 512, 512)
                    for c in range(4):
                        ch = g * 4 + c
                        acc = accs.next()
                        fmm(acc, wb, c * 128, 128, sbk)
                        sbf = stg_b.next()
                        b.copy("dve", sbf[:], acc[:], [acc.tile], [sbf.tile])
                        if ch < 6:
                            dst = qt_o[QB0 + ch * 128:QB0 + (ch + 1) * 128, tsl]
                            dtl = otile("q", "b", ch, sbk)
                        else:
                            ci_, lr_ = kloc(KB0 + (ch - 6) * 128)
                            dst = io["ksrc"][ci_][lr_:lr_ + 128, tsl]
                            dtl = otile("k", ci_, lr_, sbk)
                        rope_store(acc[:], acc.tile, sbf, 128, 1, 1, dst, dtl, cst)
                for (c0, ncol, vcol) in ((2624, 512, VB0), (3136, 256, VB0 + 512), (4416, 256, VC0)):
                    wb = load_w(c0, ncol)
                    for tt_ in range(4):
                        acc = accs.next()
                        for k in range(16):
                            b.mm(acc[:, 0:ncol], hTs[sbk][:, k, tt_ * 128:(tt_ + 1) * 128],
                                 wb[:, k, 0:ncol], k == 0, k == 15, [wb.tile, hTs[sbk].tile], [acc.tile])
                        ob = obf.next()
                        b.copy("dve", ob[:, 0:ncol], acc[:, 0:ncol], [acc.tile], [ob.tile])
                        r0 = sbk * 512 + tt_ * 128
                        b.dma("pool", io["vsrc"][sbk][tt_ * 128:(tt_ + 1) * 128, vcol:vcol + ncol], ob[:, 0:ncol], [ob.tile], [otile("v", sbk, vcol, tt_)])
                for g in range(2):
                    wb = load_w(3392 + g * 512, 512)
                    for c in range(4):
                        ch = g * 4 + c
                        acc = accs.next()
                        fmm(acc, wb, c * 128, 128, sbk)
                        qf = stg_f.next()
                        b.copy("dve", qf[:], acc[:], [acc.tile], [qf.tile])
                        sq = sqb.next()
                        b.act(sq[:], qf[:], AF.Square, [qf.tile], [sq.tile])
                        sp_ = aux.next()
                        b.mm(sp_[:], ones[:], sq[:], True, True, [ones.tile, sq.tile], [sp_.tile])
                        rs = rsb.next()
                        b.copy("dve", rs[:], sp_[:], [sp_.tile], [rs.tile])
                        b.rstd(rs, rs[:], 128)
                        gcol = 8 if ch < 6 else 9
                        b.stt("dve", qf[:], qf[:], gains[:, gcol:gcol + 1], rs[:], ALU.mult, ALU.mult,
                              [qf.tile, gains.tile, rs.tile], [qf.tile])
                        sbf = stg_b.next()
                        b.copy("dve", sbf[:], qf[:], [qf.tile], [sbf.tile])
                        if ch < 6:
                            dst = qt_o[QC0 + ch * 128:QC0 + (ch + 1) * 128, tsl]
                            dtl = otile("q", "c", ch, sbk)
                        else:
                            ci_, lr_ = kloc(KC0 + (ch - 6) * 128)
                            dst = io["ksrc"][ci_][lr_:lr_ + 128, tsl]
                            dtl = otile("k", ci_, lr_, sbk)
                        rope_store(qf[:], qf.tile, sbf, 128, 2, 2, dst, dtl, cst)
                if sbk < 3:
                    norm_transpose(x_src, (sbk + 1) * 512, 4, xdeps, io["attn_norm"], hTs[sbk + 1], tmp)
            b.pop()

        def cast_ffn_weights():
            LL = io["L"]
            for nm, src, dst, rows in (("g", io["w_gate"], io["wgb"], D), ("u", io["w_up"], io["wub"], D), ("d", io["w_down"], io["wdb"], DFF)):
                h = rows // 2
                for part in range(2):
                    b.dma("pool", dst[part * h:(part + 1) * h, :], src[part * h:(part + 1) * h, :], [], [dtile("wc", LL, nm, part)])

        def exchange():
            LL = io["L"]
            grp = [[0, 1], [2, 3], [4, 5], [6, 7]]

            def cc(src, dst, rtiles, wtile):
                o = P.op("pool", lambda e: e.collective_compute("AllGather", ALU.bypass, replica_groups=grp,
                                                                ins=[src.ap().opt()], outs=[dst.ap().opt()]),
                         reads=rtiles, writes=[wtile], dma=True)
                o.cc = True

            for i_ in range(len(KCH)):
                rt = [t for k_, t in dts.items() if k_[0] == "o" and k_[1] == LL and k_[2] == "k" and k_[3] == i_]
                assert rt
                cc(io["ksrc"][i_], io["kdst"][i_], rt, dtile("agk", LL, i_))
            for j_ in range(4):
                rt = [t for k_, t in dts.items() if k_[0] == "o" and k_[1] == LL and k_[2] == "v" and k_[3] == j_]
                assert rt
                cc(io["vsrc"][j_], io["vdst"][j_], rt, dtile("agv", LL, j_))

        def attention(mixers="ABC"):
            qt = io["qt"]
            LL = io["L"]
            qtiles = [t for k_, t in dts.items() if k_[0] == "o" and k_[1] == LL and k_[2] == "q"]
            vtiles = [dtile("agv", LL, j_) for j_ in range(4)]

            def kload(dst_buf, npart, row):
                ci_, lr = kloc(row)
                nrows = KCH[ci_][1]
                kd = io["kdst"][ci_]
                P.op("sp", lambda e: e.dma_start(out=dst_buf[0:npart, 0:TOK],
                                                 in_=kd.ap().rearrange("(a p) c -> a p c", a=2)[bass.ds(own_of(e), 1), lr:lr + npart, :].rearrange("a p c -> (a p) c")),
                     reads=[dtile("agk", LL, ci_)], writes=[dst_buf.tile], dma=True)
                P.op("sp", lambda e: e.dma_start(out=dst_buf[0:npart, TOK:2 * TOK],
                                                 in_=kd.ap().rearrange("(a p) c -> a p c", a=2)[bass.ds(par_of(e), 1), lr:lr + npart, :].rearrange("a p c -> (a p) c")),
                     reads=[dtile("agk", LL, ci_)], writes=[dst_buf.tile], dma=True)

            def vload(vbuf_, j_, vc0_, vw_):
                vd = io["vdst"][j_]
                P.op("sp", lambda e: e.dma_start(out=vbuf_[:, j_ * 4:(j_ + 1) * 4, 0:vw_],
                                                 in_=vd.ap().rearrange("(a r) c -> a r c", a=2)[bass.ds(own_of(e), 1), :, vc0_:vc0_ + vw_].rearrange("a (t p) c -> p (a t) c", p=128)),
                     reads=[vtiles[j_]], writes=[vbuf_.tile], dma=True)
                P.op("sp", lambda e: e.dma_start(out=vbuf_[:, 16 + j_ * 4:16 + (j_ + 1) * 4, 0:vw_],
                                                 in_=vd.ap().rearrange("(a r) c -> a r c", a=2)[bass.ds(par_of(e), 1), :, vc0_:vc0_ + vw_].rearrange("a (t p) c -> p (a t) c", p=128)),
                     reads=[vtiles[j_]], writes=[vbuf_.tile], dma=True)

            b.push()
            masks = b.sb([128, NMASK, 512], BF16, "masks")
            for m0 in range(0, NMASK, 4):
                b.dma("sp", masks[:, m0:m0 + 4, :], bmask_d[m0:m0 + 4].rearrange("m p t -> p m t"), [], [masks.tile])
            vbuf = b.sb([128, 32, 768], BF16, "vbuf")
            kTs = Rot([b.sb([128, S], BF16) for _ in range(2)])
            qTs = Rot([b.sb([128, TOK], BF16) for _ in range(2)])
            kr = b.sb([128, S], BF16, "kr")
            qrs = Rot([b.sb([128, TOK], BF16) for _ in range(2)])
            pTs = Rot([b.sb([128, 512], BF16) for _ in range(5)])
            sab = Rot([b.sb([128, 512], BF16) for _ in range(4)])
            rdens = Rot([b.sb([128, 512], F32) for _ in range(2)])
            ys = Rot([b.sb([128, 512], F32) for _ in range(2)])
            sT = Rot([pb[0], pb[1], pb[2], pb[7]])
            oT = Rot([pb[3], pb[4]])
            dn = Rot([pb[5], pb[6]])
            specs = []
            if "A" in mixers:
                specs.append(("A", 4, VA0, 512, 192.0 ** -0.5))
            if "B" in mixers:
                specs.append(("B", 6, VB0, 768, 128.0 ** -0.5))
            if "C" in mixers:
                specs.append(("C", 6, VC0, 256, 128.0 ** -0.5))
            for (mx, nh, vc0, vw, scale) in specs:
                for j_ in range(4):
                    vload(vbuf, j_, vc0, vw)
                if mx == "A":
                    kload(kr, 64, KAR0)
                kT = None
                for h in range(nh):
                    if mx == "A":
                        krow, qrow, vcol, yrow = KA0 + h * 128, QA0 + h * 192, h * 128, h * 128
                    elif mx == "B":
                        krow, qrow, vcol, yrow = KB0 + h * 128, QB0 + h * 128, h * 128, 512 + h * 128
                    else:
                        krow, qrow, vcol, yrow = KC0 + (h // 3) * 128, QC0 + h * 128, (h // 3) * 128, 1280 + h * 128
                    if not (mx == "C" and h % 3 != 0):
                        kT = kTs.next()
                        kload(kT, 128, krow)
                    qT = qTs.next()
                    b.dma("sp", qT[:, :], qt[qrow:qrow + 128, :], qtiles, [qT.tile])
                    if mx == "A":
                        qr = qrs.next()
                        b.dma("sp", qr[0:64, :], qt[qrow + 128:qrow + 192, :], qtiles, [qr.tile])
                    for qb in range(4):
                        qsl = slice(qb * 512, (qb + 1) * 512)
                        if mx == "B":
                            tiles = []
                            q0 = qb * 512
                            for delta in range(-1024, 1409, 128):
                                k0 = q0 + delta
                                if k0 < 0:
                                    tiles.append((S + k0, 20 + (delta + 1024) // 128))
                                elif k0 < TOK:
                                    tiles.append((k0, (delta + 1024) // 128))
                                else:
                                    tiles.append((k0, 28 + (delta - 512) // 128))
                        else:
                            tiles = [(t * 128, None) for t in range(32)]
                        pump(4)
                        o_ps = oT.next()
                        d_ps = dn.next()
                        n = len(tiles)
                        pair_den = mx != "B"
                        assert n % 2 == 0
                        pairs = {}

                        def pv(i_, col_, pT_):
                            b.mm(o_ps[:], vbuf[:, col_ // 128, vcol:vcol + 128], pT_[:], i_ == 0, i_ == n - 1,
                                 [vbuf.tile, pT_.tile], [o_ps.tile])
                            if not pair_den:
                                b.mm(d_ps[:], ones[:], pT_[:], i_ == 0, i_ == n - 1, [ones.tile, pT_.tile], [d_ps.tile])
                            elif i_ % 2 == 1:
                                ab_ = pairs[i_]
                                b.mm(d_ps[:], ones[:], ab_[:], i_ == 1, i_ == n - 1, [ones.tile, ab_.tile], [d_ps.tile])

                        pend = []
                        for i, (col, mi) in enumerate(tiles):
                            s_ps = sT.next()
                            if mx == "A":
                                b.mm(s_ps[:], kT[:, col:col + 128], qT[:, qsl], True, False, [kT.tile, qT.tile], [s_ps.tile])
                                b.mm(s_ps[:], kr[0:64, col:col + 128], qr[0:64, qsl], False, True, [kr.tile, qr.tile], [s_ps.tile])
                            else:
                                b.mm(s_ps[:], kT[:, col:col + 128], qT[:, qsl], True, True, [kT.tile, qT.tile], [s_ps.tile])
                            pT = pTs.next()
                            b.act(pT[:], s_ps[:], AF.Exp, [s_ps.tile], [pT.tile], scale=float(scale))
                            if mi is not None:
                                b.tt("dve", pT[:], pT[:], masks[:, mi, :], ALU.mult, [pT.tile, masks.tile], [pT.tile])
                            if pair_den and i % 2 == 1:
                                ab = sab.next()
                                b.tt("dve", ab[:], prev_pT[:], pT[:], ALU.add, [prev_pT.tile, pT.tile], [ab.tile])
                                pairs[i] = ab
                            prev_pT = pT
                            pend.append((i, col, pT))
                            if len(pend) > 2:
                                pv(*pend.pop(0))
                        while pend:
                            pv(*pend.pop(0))
                        rd = rdens.next()
                        b.recip(rd[:], d_ps[:], [d_ps.tile], [rd.tile])
                        y = ys.next()
                        b.tt("dve", y[:], o_ps[:], rd[:], ALU.mult, [o_ps.tile, rd.tile], [y.tile])
                        b.dma("pool", yT[yrow:yrow + 128, qsl], y[:], [y.tile], [dtile("yT", yrow // 128, qb)])
            b.pop()

        def phase3(x_res, xdeps):
            b.push()
            wo = b.sb([128, 16, D], BF16, "wo")
            drain(io["L"], "out")
            for k in range(16):
                b.dma("sp", wo[:, k, :], io["woutb"][k * 128:(k + 1) * 128, :], wtiles[(io["L"], "out")], [wo.tile])
            og = b.sb([128, 16], F32, "og")
            b.dma("sp", og[:], io["out_norm"].rearrange("(k p) -> p k", p=128), [], [og.tile], slow=True)
            ybs = Rot([b.sb([128, 16, 512], F32) for _ in range(1)])
            yn = b.sb([128, 16, 512], BF16, "yn")
            sqb = Rot([b.sb([128, 512], BF16) for _ in range(3)])
            rsb = Rot([b.sb([128, 512], F32) for _ in range(2)])
            xts = Rot([b.sb([128, D], F32) for _ in range(2)])
            accs = Rot([pb[0], pb[1], pb[2], pb[3]])
            aux = Rot([pb[4], pb[5]])
            groups = ((0, 4), (4, 10), (10, 16))
            for tb in range(4):
                tsl = slice(tb * 512, (tb + 1) * 512)
                yb = ybs.next()
                for c in range(16):
                    b.dma("sp", yb[:, c, :], yT[c * 128:(c + 1) * 128, tsl], [dtile("yT", c, tb)], [yb.tile])
                for (g0, g1) in groups:
                    sp_ = aux.next()
                    for c in range(g0, g1):
                        sq = sqb.next()
                        b.act(sq[:], yb[:, c, :], AF.Square, [yb.tile], [sq.tile])
                        b.mm(sp_[:], ones[:], sq[:], c == g0, c == g1 - 1, [ones.tile, sq.tile], [sp_.tile])
                    rs = rsb.next()
                    b.copy("dve", rs[:], sp_[:], [sp_.tile], [rs.tile])
                    b.rstd(rs, rs[:], (g1 - g0) * 128)
                    for c in range(g0, g1):
                        b.stt("dve", yn[:, c, :], yb[:, c, :], og[:, c:c + 1], rs[:], ALU.mult, ALU.mult,
                              [yb.tile, og.tile, rs.tile], [yn.tile])
                for tt_ in range(4):
                    xt = xts.next()
                    r0 = tb * 512 + tt_ * 128
                    b.dma("sp", xt[:], x_res[r0:r0 + 128, :], [xdeps(r0 // 128)] if xdeps else [], [xt.tile])
                    for cg in range(4):
                        acc = accs.next()
                        for k in range(16):
                            b.mm(acc[:], yn[:, k, tt_ * 128:(tt_ + 1) * 128], wo[:, k, cg * 512:(cg + 1) * 512],
                                 k == 0, k == 15, [yn.tile, wo.tile], [acc.tile])
                        b.tt("dve", xt[:, cg * 512:(cg + 1) * 512], acc[:], xt[:, cg * 512:(cg + 1) * 512], ALU.add,
                             [acc.tile, xt.tile], [xt.tile])
                    b.dma("pool", x1s[r0:r0 + 128, :], xt[:], [xt.tile], [dtile("x1", r0 // 128)])
            b.pop()

        def ffn(do_final, x_dst, dkey):
            b.push()
            FG = 256
            h2T = b.sb([128, 16, 512], BF16, "h2T")
            x1k = [b.sb([128, D], F32) for _ in range(4)]
            aT = b.sb([128, 44, 512], BF16, "aT")
            wgs = Rot([b.sb([128, 16, FG], BF16) for _ in range(2)])
            wus = Rot([b.sb([128, 16, FG], BF16) for _ in range(2)])
            wds = Rot([b.sb([128, 44, 256], BF16) for _ in range(2)])
            tmp = dict(gb=b.sb([128, D], F32), xts=None, xn=Rot([b.sb([128, D], BF16) for _ in range(1)]),
                       sss=Rot([b.sb([128, 1], F32) for _ in range(2)]), tps=Rot([pb[6], pb[7]]))
            sgs = Rot([b.sb([128, 512], F32) for _ in range(2)])
            gps = Rot([pb[0], pb[1]])
            ups = Rot([pb[2], pb[3]])
            dps = Rot([pb[4], pb[5]])
            LL = io["L"]
            wgb, wub, wdb = io["wgb"], io["wub"], io["wdb"]
            drain(LL, "d")
            cg_t, cu_t, cd_t = wtiles[(LL, "g")], wtiles[(LL, "u")], wtiles[(LL, "d")]
            for tb in range(4):
                norm_transpose(x1s, tb * 512, 4, lambda i: dtile("x1", i), io["ffn_norm"], h2T, tmp, keep=x1k)
                for fg in range(DFF // FG):
                    pump(1)
                    wg = wgs.next()
                    wu = wus.next()
                    b.dma("sp", wg[:, :, :], wgb[fg], cg_t[fg * 4:(fg + 1) * 4], [wg.tile])
                    b.dma("sp", wu[:, :, :], wub[fg], cu_t[fg * 4:(fg + 1) * 4], [wu.tile])
                    for c in range(FG // 128):
                        f = fg * (FG // 128) + c
                        g_ps = gps.next()
                        u_ps = ups.next()
                        for k in range(16):
                            b.mm(g_ps[:], wg[:, k, c * 128:(c + 1) * 128], h2T[:, k, :], k == 0, k == 15,
                                 [wg.tile, h2T.tile], [g_ps.tile])
                        for k in range(16):
                            b.mm(u_ps[:], wu[:, k, c * 128:(c + 1) * 128], h2T[:, k, :], k == 0, k == 15,
                                 [wu.tile, h2T.tile], [u_ps.tile])
                        sg = sgs.next()
                        b.act(sg[:], g_ps[:], AF.Silu, [g_ps.tile], [sg.tile])
                        b.tt("dve", aT[:, f, :], u_ps[:], sg[:], ALU.mult, [u_ps.tile, sg.tile], [aT.tile])
                for cg in range(D // 256):
                    pump(1)
                    wd = wds.next()
                    for f0 in (0, 22):
                        b.dma("sp", wd[:, f0:f0 + 22, :], wdb[cg, :, f0:f0 + 22, :], cd_t[cg * 11:(cg + 1) * 11], [wd.tile])
                    for tt_ in range(4):
                        acc = dps.next()
                        for f in range(44):
                            b.mm(acc[:, 0:256], aT[:, f, tt_ * 128:(tt_ + 1) * 128], wd[:, f, :], f == 0, f == 43,
                                 [aT.tile, wd.tile], [acc.tile])
                        xk = x1k[tt_]
                        b.tt("dve", xk[:, cg * 256:(cg + 1) * 256], acc[:, 0:256], xk[:, cg * 256:(cg + 1) * 256], ALU.add,
                             [acc.tile, xk.tile], [xk.tile])
                for tt_ in range(4):
                    xk = x1k[tt_]
                    r0 = tb * 512 + tt_ * 128
                    if do_final:
                        gb = tmp["gb"]
                        if tt_ == 0:
                            b.dma("sp", gb[:], final_norm_d.partition_broadcast(128), [], [gb.tile])
                        xn = tmp["xn"].next()
                        ss = tmp["sss"].next()
                        b.act(xn[:], xk[:], AF.Square, [xk.tile], [xn.tile, ss.tile], accum_out=ss[:, 0:1])
                        b.rstd(ss, ss[:, 0:1], D)
                        b.stt("dve", xk[:], xk[:], ss[:, 0:1], gb[:], ALU.mult, ALU.mult, [xk.tile, ss.tile, gb.tile], [xk.tile])
                    t_ = dtile(dkey, r0 // 128)
                    b.dma("pool", x_dst[r0:r0 + 128, :], xk[:], [xk.tile], [t_])
                    if do_final:
                        out_tiles.append(t_)
            b.pop()

        for l in range(2):
            io = IO[l]
            if l == 0:
                phase1(x_ext, None)
            else:
                phase1(x2s, lambda i: dtile("x2", i))
            exchange()
            attention()
            if l == 0:
                phase3(x_ext, None)
                ffn(False, x2s, "x2")
            else:
                phase3(x2s, lambda i: dtile("x2", i))
                ffn(True, x_out, "xo")

        P.op("sp", lambda e: e.nop(), reads=out_tiles)
        P.op("pool", lambda e: e.nop(), reads=out_tiles)
        b.P.finalize(st)
    return nc


def _rope_tables():
    f32 = np.float32
    pos = np.arange(S, dtype=np.int32)
    row = pos // 64
    col = pos % 64

    def ang(p, dim):
        inv = (f32(10000.0) ** (-np.arange(0, dim, 2, dtype=f32) / f32(dim))).astype(f32)
        return p.astype(f32)[:, None] * inv[None, :]

    out = np.zeros((6, 128, S), f32)
    aa = ang(pos, 64)
    ab = ang(pos, 128)
    ar = ang(row, 64)
    ac = ang(col, 64)
    A = np.concatenate([aa, aa], 1)
    Bm = np.concatenate([ab, ab], 1)
    C = np.concatenate([ar, ar, ac, ac], 1)
    out[0, :64] = np.cos(A).T
    out[1, :64] = np.sin(A).T
    out[2] = np.cos(Bm).T
    out[3] = np.sin(Bm).T
    out[4] = np.cos(C).T
    out[5] = np.sin(C).T
    return out


def _rot_mats():
    r = np.zeros((3, 128, 128), np.float32)

    def fill(m, off, n):
        h = n // 2
        for j in range(h):
            m[off + j + h, off + j] = -1.0
        for j in range(h, n):
            m[off + j - h, off + j] = 1.0

    fill(r[0], 0, 64)
    fill(r[1], 0, 128)
    fill(r[2], 0, 64)
    fill(r[2], 64, 64)
    return r


def _bmasks(hf):
    i = np.arange(128)[:, None]
    j = np.arange(512)[None, :]
    own = np.zeros((20, 128, 512), np.float32)
    for m in range(20):
        r = (-1024 + 128 * m) + i - j
        a = np.abs(r)
        own[m] = (a <= 64).astype(np.float32) + ((a <= 256) & (r % 4 == 0)) + ((a <= 1024) & (r % 16 == 0))
    out = np.zeros((NMASK, 128, 512), np.float32)
    out[0:20] = own
    if hf == 1:
        out[20:28] = own[0:8]
    else:
        out[28:36] = own[12:20]
    return out


_CACHE = {}


def _prog():
    if "F" not in _CACHE:
        _CACHE["F"] = build_program()
    return _CACHE["F"]


_W_KEYS = ("attn_norm", "w_in", "a_q_norm", "a_kv_norm", "a_w_uq", "a_w_ukv", "c_q_norm", "c_k_norm",
           "out_norm", "w_out", "ffn_norm", "w_gate", "w_up", "w_down")


def kernel(**inputs):
    inp = {k: np.asarray(v) for k, v in inputs.items()}
    x = inp["x"]
    cs = _rope_tables()
    rots = _rot_mats()
    cores = list(range(8))
    bms = [_bmasks(0).astype(ml_dtypes.bfloat16), _bmasks(1).astype(ml_dtypes.bfloat16)]
    wts = {}
    for l in range(2):
        for k in _W_KEYS:
            wts["%s%d" % (k, l)] = np.ascontiguousarray(inp[k][l])
    fn = np.ascontiguousarray(inp["final_norm"])
    maps = []
    for c in cores:
        hf = c % 2
        m = dict(x=np.ascontiguousarray(x[c // 2, hf * TOK:(hf + 1) * TOK]), rots=rots,
                 cs=np.ascontiguousarray(cs[:, :, hf * TOK:(hf + 1) * TOK]), bmask=bms[hf], final_norm=fn)
        m.update(wts)
        maps.append(m)
    res = run_bass_kernel_spmd(_prog(), maps, core_ids=cores).results
    out = np.empty((4, S, D), np.float32)
    for c in cores:
        out[c // 2, (c % 2) * TOK:(c % 2 + 1) * TOK] = res[c]["x_out"]
    return out
```

```python
import numpy as np
from contextlib import ExitStack
import ml_dtypes
import concourse.bass as bass
import concourse.mybir as mybir
from concourse.bass_utils import run_bass_kernel_spmd

F32 = mybir.dt.float32
BF16 = mybir.dt.bfloat16
AF = mybir.ActivationFunctionType
ALU = mybir.AluOpType

SEM_CHUNK = 30000
N_DMA_SLOTS = 6
DMA_SLOTS = {"pool": 16, "sp": 8}

D = 2048
S = 4096
TOK = 2048
DFF = 5632
INW = 4672
EPS = 1e-6
NQROWS = 4 * 192 + 768 + 768
NKROWS = 512 + 64 + 768 + 256
NVCOLS = 512 + 768 + 256
NMASK = 36
QA0, QB0, QC0 = 0, 768, 1536
KA0, KAR0, KB0, KC0 = 0, 512, 576, 1344
VA0, VB0, VC0 = 0, 512, 1280


class Tile:
    __slots__ = ("w", "r", "name", "excl")

    def __init__(self, name="", excl=False):
        self.w = None
        self.r = []
        self.name = name
        self.excl = excl


class Op:
    __slots__ = ("eng", "fn", "deps", "dma", "needed", "ev", "slot_prev", "cc")

    def __init__(self, eng, fn, dma):
        self.eng = eng
        self.fn = fn
        self.dma = dma
        self.cc = False
        self.deps = []
        self.needed = False
        self.ev = None
        self.slot_prev = None


class Prog:
    ENGS = ("pe", "act", "dve", "pool", "sp")

    def __init__(self, nc):
        self.nc = nc
        self.ops = {e: [] for e in self.ENGS}

    def op(self, eng, fn, reads=(), writes=(), dma=False):
        o = Op(eng, fn, dma)
        deps = []
        xr = [t for t in reads if t.excl]
        if xr:
            reads = [t for t in reads if not t.excl]
            writes = list(writes) + [t for t in xr if t not in writes]
        for t in reads:
            if t.w is not None:
                deps.append(t.w)
        for t in writes:
            if t.w is not None:
                deps.append(t.w)
            deps.extend(t.r)
        seen = set()
        for d in deps:
            if id(d) in seen:
                continue
            seen.add(id(d))
            if d.eng == "pe" and eng == "pe" and not d.dma and not dma:
                continue
            o.deps.append(d)
        for t in reads:
            t.r.append(o)
        for t in writes:
            t.w = o
            t.r = []
        self.ops[eng].append(o)
        return o

    def finalize(self, stack):
        nc = self.nc
        for e in self.ENGS:
            for o in self.ops[e]:
                for d in o.deps:
                    d.needed = True
        sems = {}

        def getsem(key):
            if key not in sems:
                sems[key] = stack.enter_context(nc.semaphore("s_%s_%s_%s" % key))
            return sems[key]

        ncc = [0]
        for e in self.ENGS:
            cnt = 0
            dcnt = 0
            nsl = DMA_SLOTS.get(e, N_DMA_SLOTS)
            slot_cnt = [0] * nsl
            slot_last = [None] * nsl
            for o in self.ops[e]:
                if o.cc:
                    ncc[0] += 1
                    o.ev = (getsem((e, "cc", ncc[0])), 1)
                elif o.dma:
                    s = dcnt % nsl
                    dcnt += 1
                    slot_cnt[s] += 1
                    assert slot_cnt[s] * 16 < 60000, "dma sem overflow"
                    o.slot_prev = slot_last[s]
                    o.ev = (getsem((e, "d", s)), slot_cnt[s] * 16)
                    slot_last[s] = o.ev
                elif o.needed:
                    cnt += 1
                    o.ev = (getsem((e, "c", (cnt - 1) // SEM_CHUNK)), (cnt - 1) % SEM_CHUNK + 1)
        block = stack.enter_context(nc.Block())
        ops = self.ops

        def emit(e, eng):
            known = {}
            for o in ops[e]:
                waits = [d.ev for d in o.deps]
                if o.dma and o.slot_prev is not None:
                    waits.append(o.slot_prev)
                for (sem, val) in waits:
                    k = id(sem)
                    if known.get(k, 0) >= val:
                        continue
                    eng.wait_ge(sem, val)
                    known[k] = val
                ins = o.fn(eng)
                if o.cc:
                    ins.then_inc(o.ev[0])
                elif o.dma:
                    ins.then_inc(o.ev[0], 16)
                elif o.needed:
                    ins.then_inc(o.ev[0], 1)

        @block.sync
        def _(eng):
            emit("sp", eng)

        @block.tensor
        def _(eng):
            emit("pe", eng)

        @block.scalar
        def _(eng):
            emit("act", eng)

        @block.vector
        def _(eng):
            emit("dve", eng)

        @block.gpsimd
        def _(eng):
            emit("pool", eng)


class Buf:
    def __init__(self, t, name, fence=None):
        self.t = t
        self.tile = Tile(name)
        self.tile.w = fence

    def __getitem__(self, k):
        return self.t[k]


class Bld:
    def __init__(self, nc, st):
        self.nc = nc
        self.st = st
        self.P = Prog(nc)
        self.n = 0
        self.fence = None
        self.scopes = []

    def push(self):
        self.scopes.append((self.st, []))
        self.st = ExitStack()

    def pop(self):
        outer, _ = self.scopes[-1]
        bufs = self.scopes[-1][1]
        self.fence = self.P.op("sp", lambda e: e.nop(), writes=[bf_.tile for bf_ in bufs])
        self.st.close()
        self.st = outer
        self.scopes.pop()

    def sb(self, shape, dt, name=None):
        self.n += 1
        name = ("%s_%d" % (name, self.n)) if name else "sb%d" % self.n
        bf_ = Buf(self.st.enter_context(self.nc.sbuf_tensor(name, list(shape), dt)), name, self.fence)
        if self.scopes:
            self.scopes[-1][1].append(bf_)
        return bf_

    def ps(self, shape, dt, name=None):
        self.n += 1
        name = name or "ps%d" % self.n
        bf_ = Buf(self.st.enter_context(self.nc.psum_tensor(name, list(shape), dt)), name)
        bf_.tile.excl = True
        return bf_

    def dma(self, q, out, in_, reads, writes, slow=False):
        if slow:
            self.P.op(q, lambda e: e.dma_start(out=out, in_=in_, allow_slow_non_contiguous=True),
                      reads=reads, writes=writes, dma=True)
        else:
            self.P.op(q, lambda e: e.dma_start(out=out, in_=in_), reads=reads, writes=writes, dma=True)

    def mm(self, out, lhsT, rhs, start, stop, reads, writes):
        self.P.op("pe", lambda e: e.matmul(out, lhsT=lhsT, rhs=rhs, start=start, stop=stop),
                  reads=reads, writes=writes)

    def tr(self, out, in_, ident, reads, writes):
        self.P.op("pe", lambda e: e.transpose(out, in_, ident), reads=reads, writes=writes)

    def act(self, out, in_, func, reads, writes, scale=None, bias=None, accum_out=None, eng="act"):
        kw = {}
        if scale is not None:
            kw["scale"] = scale
        if bias is not None:
            kw["bias"] = bias
        if accum_out is not None:
            kw["accum_out"] = accum_out
        self.P.op(eng, lambda e: e.activation(out=out, in_=in_, func=func, **kw), reads=reads, writes=writes)

    def ts(self, eng, out, in0, s1, s2, op0, op1, reads, writes):
        if op1 is None:
            self.P.op(eng, lambda e: e.tensor_scalar(out=out, in0=in0, scalar1=s1, scalar2=None, op0=op0),
                      reads=reads, writes=writes)
        else:
            self.P.op(eng, lambda e: e.tensor_scalar(out=out, in0=in0, scalar1=s1, scalar2=s2, op0=op0, op1=op1),
                      reads=reads, writes=writes)

    def tt(self, eng, out, in0, in1, op, reads, writes):
        self.P.op(eng, lambda e: e.tensor_tensor(out=out, in0=in0, in1=in1, op=op), reads=reads, writes=writes)

    def stt(self, eng, out, in0, scalar, in1, op0, op1, reads, writes):
        self.P.op(eng, lambda e: e.scalar_tensor_tensor(out=out, in0=in0, scalar=scalar, in1=in1, op0=op0, op1=op1),
                  reads=reads, writes=writes)

    def copy(self, eng, out, in_, reads, writes):
        if eng == "act":
            self.P.op("act", lambda e: e.copy(out=out, in_=in_), reads=reads, writes=writes)
        else:
            self.P.op(eng, lambda e: e.tensor_copy(out=out, in_=in_), reads=reads, writes=writes)

    def recip(self, out, in_, reads, writes):
        self.P.op("dve", lambda e: e.reciprocal(out=out, in_=in_), reads=reads, writes=writes)

    def memset(self, eng, ap, val, writes):
        self.P.op(eng, lambda e: e.memset(ap, val), writes=writes)

    def rstd(self, buf, ap, n):
        self.ts("dve", ap, ap, 1.0 / n, EPS, ALU.mult, ALU.add, [buf.tile], [buf.tile])
        self.act(ap, ap, AF.Sqrt, [buf.tile], [buf.tile])
        self.recip(ap, ap, [buf.tile], [buf.tile])


class Rot:
    def __init__(self, items):
        self.items = items
        self.i = 0

    def next(self):
        it = self.items[self.i % len(self.items)]
        self.i += 1
        return it


WIN_GROUPS = ((0, 512), (512, 512), (1024, 64), (1088, 512), (1600, 512), (2112, 512),
              (2624, 512), (3136, 256), (4416, 256), (3392, 512), (3904, 512))
WIN_GIDX = {c0: i for i, (c0, n) in enumerate(WIN_GROUPS)}

KCH = ((0, 512), (512, 64), (576, 512), (1088, 256), (1344, 256))


def kloc(row):
    for i, (r0, n) in enumerate(KCH):
        if r0 <= row < r0 + n:
            return i, row - r0
    raise ValueError(row)


def build_program():
    nc = bass.Bass("TRN2", target_bir_lowering=False)

    def din(name, shape, dt=F32):
        return nc.dram_tensor(name, list(shape), dt, kind="ExternalInput").ap()

    def dout(name, shape, dt=F32):
        return nc.dram_tensor(name, list(shape), dt, kind="ExternalOutput").ap()

    def dint(name, shape, dt=BF16):
        return nc.dram_tensor(name, list(shape), dt, kind="Internal").ap()

    st = ExitStack()
    with st:
        b = Bld(nc, st)
        P = b.P
        dts = {}

        def dtile(*key):
            if key not in dts:
                dts[key] = Tile(str(key))
            return dts[key]

        x_ext = din("x", [TOK, D])
        consts = din("rots", [3, 128, 128])
        cs = din("cs", [6, 128, TOK])
        bmask_d = din("bmask", [NMASK, 128, 512], BF16)
        final_norm_d = din("final_norm", [D])
        x_out = dout("x_out", [TOK, D])
        WSH = dict(attn_norm=[D], w_in=[D, INW], a_q_norm=[512], a_kv_norm=[512], a_w_uq=[512, 768],
                   a_w_ukv=[512, 1024], c_q_norm=[128], c_k_norm=[128], out_norm=[D], w_out=[D, D],
                   ffn_norm=[D], w_gate=[D, DFF], w_up=[D, DFF], w_down=[DFF, D])
        IO = []
        for l in range(2):
            d_ = {k: din("%s%d" % (k, l), shp) for k, shp in WSH.items()}
            d_["qt"] = dint("qt%d" % l, [NQROWS, TOK])
            d_["ksrc"] = [nc.dram_tensor("ks%d_%d" % (l, i), [n, TOK], BF16) for i, (r0, n) in enumerate(KCH)]
            d_["kdst"] = [nc.dram_tensor("kd%d_%d" % (l, i), [2 * n, TOK], BF16) for i, (r0, n) in enumerate(KCH)]
            d_["vsrc"] = [nc.dram_tensor("vs%d_%d" % (l, j), [512, NVCOLS], BF16) for j in range(4)]
            d_["vdst"] = [nc.dram_tensor("vd%d_%d" % (l, j), [1024, NVCOLS], BF16) for j in range(4)]
            d_["winb"] = dint("winb%d" % l, [len(WIN_GROUPS), 128, 16, 512])
            d_["woutb"] = dint("woutb%d" % l, [D, D])
            d_["wgb"] = dint("wgb%d" % l, [DFF // 256, 128, 16, 256])
            d_["wub"] = dint("wub%d" % l, [DFF // 256, 128, 16, 256])
            d_["wdb"] = dint("wdb%d" % l, [D // 256, 128, 44, 256])
            d_["L"] = l
            IO.append(d_)
        yT = dint("yT_s", [D, TOK], F32)
        x1s = dint("x1_s", [TOK, D], F32)
        x2s = dint("x2_s", [TOK, D], F32)
        io = IO[0]
        own_cache = {}

        def own_of(e):
            k = id(e)
            if k not in own_cache:
                own_cache[k] = e.snap(e.partition_id() % 2, min_val=0, max_val=1)
            return own_cache[k]

        par_cache = {}

        def par_of(e):
            k = id(e)
            if k not in par_cache:
                par_cache[k] = e.snap((e.partition_id() + 1) % 2, min_val=0, max_val=1)
            return par_cache[k]

        out_tiles = []

        identf = b.sb([128, 128], F32, "identf")
        ident = b.sb([128, 128], BF16, "ident")
        ones = b.sb([128, 128], BF16, "ones")
        b.memset("pool", identf[:], 1.0, [identf.tile])
        P.op("pool", lambda e: e.affine_select(out=identf[:], in_=identf[:], pattern=[[-1, 128]],
                                               compare_op=ALU.is_equal, fill=0.0, base=0, channel_multiplier=1),
             reads=[identf.tile], writes=[identf.tile])
        b.copy("dve", ident[:], identf[:], [identf.tile], [ident.tile])
        b.memset("dve", ones[:], 1.0, [ones.tile])
        rots = b.sb([128, 3, 128], BF16, "rots_sb")
        for r_ in range(3):
            b.dma("pool", rots[:, r_, :], consts[r_], [], [rots.tile])

        pb = [b.ps([128, 512], F32, "pb%d" % i) for i in range(8)]

        PCH = 1024
        pc_f = Rot([b.sb([128, PCH], F32) for _ in range(2)])
        pc_b = Rot([b.sb([128, PCH], BF16) for _ in range(2)])
        work = []
        wtiles = {}

        def add_precast(l, name, src, dst, ceng="pool"):
            R_, C_ = src.shape
            n = R_ * C_ // 128
            assert n % PCH == 0
            sv = src.rearrange("(p a) c -> p (a c)", p=128)
            dv = dst.rearrange("(p a) c -> p (a c)", p=128)
            tl = []
            for i in range(n // PCH):
                t = dtile("wc", l, name, i)
                tl.append(t)

                def step(i=i, t=t):
                    f = pc_f.next()
                    g = pc_b.next()
                    b.dma("sp", f[:], sv[:, i * PCH:(i + 1) * PCH], [], [f.tile])
                    b.copy(ceng, g[:], f[:], [f.tile], [g.tile])
                    b.dma("pool", dv[:, i * PCH:(i + 1) * PCH], g[:], [g.tile], [t])
                work.append(((l, name), step))
            wtiles[(l, name)] = tl

        def add_precast_blocked(l, name, srcv, dstb, ngroups, ninner):
            tl = []
            for g_ in range(ngroups):
                for i4 in range(ninner // 4):
                    t = dtile("wc", l, name, g_, i4)
                    tl.append(t)

                    def step(g_=g_, i4=i4, t=t):
                        f = pc_f.next()
                        g = pc_b.next()
                        b.dma("sp", f[:].rearrange("p (a c) -> p a c", a=4),
                              srcv[:, i4 * 4:(i4 + 1) * 4, g_ * 256:(g_ + 1) * 256], [], [f.tile])
                        b.copy("pool", g[:], f[:], [f.tile], [g.tile])
                        b.dma("pool", dstb[g_, :, i4 * 4:(i4 + 1) * 4, :], g[:].rearrange("p (a c) -> p a c", a=4), [g.tile], [t])
                    work.append(((l, name), step))
            wtiles[(l, name)] = tl

        def add_precast_win(l, src, dstb, ceng):
            srcv = src.rearrange("(k p) n -> p k n", p=128)
            tl = {}
            for g_, (c0, ncol) in enumerate(WIN_GROUPS):
                rpc = PCH // ncol
                tl[g_] = []
                for j in range(16 // rpc):
                    t = dtile("wc", l, "in", g_, j)
                    tl[g_].append(t)

                    def step(g_=g_, j=j, t=t, c0=c0, ncol=ncol, rpc=rpc):
                        f = pc_f.next()
                        g = pc_b.next()
                        b.dma("sp", f[:].rearrange("p (a c) -> p a c", a=rpc), srcv[:, j * rpc:(j + 1) * rpc, c0:c0 + ncol], [], [f.tile])
                        b.copy(ceng, g[:], f[:], [f.tile], [g.tile])
                        b.dma("pool", dstb[g_, :, j * rpc:(j + 1) * rpc, 0:ncol], g[:].rearrange("p (a c) -> p a c", a=rpc), [g.tile], [t])
                    work.append(((l, "in", g_), step))
            wtiles[(l, "in")] = tl

        def drain_tag(tag):
            while any(t_ == tag for t_, _ in work):
                work.pop(0)[1]()

        def pump(n):
            for _ in range(n):
                if work:
                    work.pop(0)[1]()

        def drain(l, name):
            while any(tag[:2] == (l, name) for tag, _ in work):
                work.pop(0)[1]()

        for l_ in range(2):
            add_precast_win(l_, IO[l_]["w_in"], IO[l_]["winb"], "dve" if l_ == 0 else "pool")
            add_precast(l_, "out", IO[l_]["w_out"], IO[l_]["woutb"])
            add_precast_blocked(l_, "g", IO[l_]["w_gate"].rearrange("(k p) n -> p k n", p=128), IO[l_]["wgb"], DFF // 256, 16)
            add_precast_blocked(l_, "u", IO[l_]["w_up"].rearrange("(k p) n -> p k n", p=128), IO[l_]["wub"], DFF // 256, 16)
            add_precast_blocked(l_, "d", IO[l_]["w_down"].rearrange("(f p) n -> p f n", p=128), IO[l_]["wdb"], D // 256, 44)

        def nt_alloc(nx):
            return dict(gb=b.sb([128, D], F32), xts=Rot([b.sb([128, D], F32) for _ in range(nx)]),
                        xn=Rot([b.sb([128, D], BF16) for _ in range(2)]),
                        sss=Rot([b.sb([128, 1], F32) for _ in range(2)]), tps=Rot([pb[6], pb[7]]))

        def norm_transpose(x_src, row0, ntile, deps, g_dram, hT, tmp, keep=None, per_tile=None):
            gb = tmp["gb"]
            b.dma("sp", gb[:], g_dram.partition_broadcast(128), [], [gb.tile])
            for i in range(ntile):
                xt = keep[i] if keep is not None else tmp["xts"].next()
                xn = tmp["xn"].next()
                ss = tmp["sss"].next()
                r0 = row0 + i * 128
                b.dma("sp", xt[:], x_src[r0:r0 + 128, :], [deps(r0 // 128)] if deps else [], [xt.tile])
                b.act(xn[:], xt[:], AF.Square, [xt.tile], [xn.tile, ss.tile], accum_out=ss[:, 0:1])
                b.rstd(ss, ss[:, 0:1], D)
                b.stt("dve", xn[:], xt[:], ss[:, 0:1], gb[:], ALU.mult, ALU.mult, [xt.tile, ss.tile, gb.tile], [xn.tile])
                for half in range(2):
                    tp = tmp["tps"].next()
                    tpv = tp[:].bitcast(BF16)
                    for j in range(8):
                        k = half * 8 + j
                        b.tr(tpv[:, j * 128:(j + 1) * 128], xn[:, k * 128:(k + 1) * 128], ident[:],
                             [xn.tile, ident.tile], [tp.tile])
                    b.copy("dve", hT[:, half * 8:(half + 1) * 8, i * 128:(i + 1) * 128],
                           tpv.rearrange("p (k t) -> p k t", k=8), [tp.tile], [hT.tile])
                if per_tile is not None:
                    per_tile()

        def phase1(x_src, xdeps):
            b.push()
            w_in = io["w_in"]
            hTs = [b.sb([128, 16, 512], BF16, "hT") for _ in range(4)]
            tmp = dict(gb=b.sb([128, D], F32), xts=Rot([b.sb([128, D], F32) for _ in range(1)]),
                       xn=Rot([b.sb([128, D], BF16) for _ in range(1)]),
                       sss=Rot([b.sb([128, 1], F32) for _ in range(2)]), tps=Rot([pb[6], pb[7]]))
            norm_transpose(x_src, 0, 4, xdeps, io["attn_norm"], hTs[0], tmp)
            wuq = b.sb([128, 4, 768], BF16, "wuq")
            wukv = b.sb([128, 4, 1024], BF16, "wukv")
            for k in range(4):
                b.dma("pool", wuq[:, k, :], io["a_w_uq"][k * 128:(k + 1) * 128, :], [], [wuq.tile])
                b.dma("pool", wukv[:, k, :], io["a_w_ukv"][k * 128:(k + 1) * 128, :], [], [wukv.tile])
            gains = b.sb([128, 10], F32, "gains")
            b.dma("sp", gains[:, 0:4], io["a_q_norm"].rearrange("(k p) -> p k", p=128), [], [gains.tile], slow=True)
            b.dma("sp", gains[:, 4:8], io["a_kv_norm"].rearrange("(k p) -> p k", p=128), [], [gains.tile], slow=True)
            b.dma("sp", gains[:, 8:9], io["c_q_norm"].rearrange("(k p) -> p k", p=128), [], [gains.tile], slow=True)
            b.dma("sp", gains[:, 9:10], io["c_k_norm"].rearrange("(k p) -> p k", p=128), [], [gains.tile], slow=True)
            winb = io["winb"]
            win_t = wtiles[(io["L"], "in")]
            LLw = io["L"]
            wbufs = Rot([b.sb([128, 16, 512], BF16) for _ in range(2)])
            csts = Rot([b.sb([128, 6, 512], F32) for _ in range(1)])
            accs = Rot([pb[0], pb[1], pb[2]])
            aux = Rot([pb[3], pb[4]])
            stg_f = Rot([b.sb([128, 512], F32) for _ in range(4)])
            stg_b = Rot([b.sb([128, 512], BF16) for _ in range(4)])
            obf = Rot([b.sb([128, 512], BF16) for _ in range(4)])
            sqb = Rot([b.sb([128, 512], BF16) for _ in range(3)])
            rsb = Rot([b.sb([128, 512], F32) for _ in range(2)])
            t1b = Rot([b.sb([128, 512], F32) for _ in range(4)])
            cq = b.sb([128, 4, 512], F32, "cq")
            cqn = b.sb([128, 4, 512], BF16, "cqn")
            qt_o = io["qt"]
            LL = io["L"]

            def kdst_ap(row, tsl_):
                i_, lr = kloc(row)
                return io["ksrc"][i_][lr:lr + 128, tsl_] if True else None

            def otile(kind, *key):
                return dtile("o", LL, kind, *key)

            def load_w(c0, ncol):
                wb = wbufs.next()
                gi = WIN_GIDX[c0]
                assert WIN_GROUPS[gi][1] == ncol
                drain_tag((LLw, "in", gi))
                if gi + 1 < len(WIN_GROUPS):
                    drain_tag((LLw, "in", gi + 1))
                for k8 in range(2):
                    b.dma("sp", wb[:, k8 * 8:(k8 + 1) * 8, 0:ncol], winb[gi, :, k8 * 8:(k8 + 1) * 8, 0:ncol], win_t[gi], [wb.tile])
                pump(1)
                return wb

            def fmm(acc, wb, off, ncol, sbk):
                for k in range(16):
                    b.mm(acc[0:ncol, :], wb[:, k, off:off + ncol], hTs[sbk][:, k, :],
                         k == 0, k == 15, [wb.tile, hTs[sbk].tile], [acc.tile])

            def rope_store(src_f, src_tile, srcb, npart, ridx, tab, dst_ap, dst_tile, cst):
                rp = aux.next()
                b.mm(rp[0:npart, :], rots[0:npart, ridx, 0:npart], srcb[0:npart, :], True, True,
                     [rots.tile, srcb.tile], [rp.tile])
                t1 = t1b.next()
                ob = obf.next()
                b.tt("dve", t1[0:npart, :], src_f, cst[0:npart, 2 * tab, :], ALU.mult, [src_tile, cst.tile], [t1.tile])
                t2 = t1b.next()
                b.tt("dve", t2[0:npart, :], rp[0:npart, :], cst[0:npart, 2 * tab + 1, :], ALU.mult, [rp.tile, cst.tile], [t2.tile])
                b.tt("dve", ob[0:npart, :], t1[0:npart, :], t2[0:npart, :], ALU.add, [t1.tile, t2.tile], [ob.tile])
                b.dma("pool", dst_ap, ob[0:npart, :], [ob.tile], [dst_tile])

            for sbk in range(4):
                tsl = slice(sbk * 512, (sbk + 1) * 512)
                cst = csts.next()
                for r_ in range(6):
                    b.dma("sp", cst[:, r_, :], cs[r_, :, tsl], [], [cst.tile])
                for which in range(2):
                    wb = load_w(which * 512, 512)
                    for c in range(4):
                        acc = accs.next()
                        fmm(acc, wb, c * 128, 128, sbk)
                        b.copy("dve", cq[:, c, :], acc[:], [acc.tile], [cq.tile])
                    sp_ = aux.next()
                    for c in range(4):
                        sq = sqb.next()
                        b.act(sq[:], cq[:, c, :], AF.Square, [cq.tile], [sq.tile])
                        b.mm(sp_[:], ones[:], sq[:], c == 0, c == 3, [ones.tile, sq.tile], [sp_.tile])
                    rs = rsb.next()
                    b.copy("dve", rs[:], sp_[:], [sp_.tile], [rs.tile])
                    b.rstd(rs, rs[:], 512)
                    for c in range(4):
                        b.stt("dve", cqn[:, c, :], cq[:, c, :], gains[:, which * 4 + c:which * 4 + c + 1], rs[:],
                              ALU.mult, ALU.mult, [cq.tile, gains.tile, rs.tile], [cqn.tile])
                    if which == 0:
                        for h in range(4):
                            acc = accs.next()
                            for k in range(4):
                                b.mm(acc[:], wuq[:, k, h * 192:h * 192 + 128], cqn[:, k, :], k == 0, k == 3,
                                     [wuq.tile, cqn.tile], [acc.tile])
                            ob = obf.next()
                            b.copy("dve", ob[:], acc[:], [acc.tile], [ob.tile])
                            b.dma("pool", qt_o[QA0 + h * 192:QA0 + h * 192 + 128, tsl], ob[:], [ob.tile], [otile("q", "a", h, sbk)])
                            acc = accs.next()
                            for k in range(4):
                                b.mm(acc[0:64, :], wuq[:, k, h * 192 + 128:h * 192 + 192], cqn[:, k, :], k == 0, k == 3,
                                     [wuq.tile, cqn.tile], [acc.tile])
                            sbf = stg_b.next()
                            b.copy("dve", sbf[0:64, :], acc[0:64, :], [acc.tile], [sbf.tile])
                            rope_store(acc[0:64, :], acc.tile, sbf, 64, 0, 0,
                                       qt_o[QA0 + h * 192 + 128:QA0 + h * 192 + 192, tsl], otile("q", "ar", h, sbk), cst)
                    else:
                        for h in range(4):
                            acc = accs.next()
                            for k in range(4):
                                b.mm(acc[:], wukv[:, k, h * 256:h * 256 + 128], cqn[:, k, :], k == 0, k == 3,
                                     [wukv.tile, cqn.tile], [acc.tile])
                            ob = obf.next()
                            b.copy("dve", ob[:], acc[:], [acc.tile], [ob.tile])
                            b.dma("pool", io["ksrc"][0][h * 128:(h + 1) * 128, tsl], ob[:], [ob.tile], [otile("k", 0, h, sbk)])
                        wv4 = wukv[:, :, :].rearrange("p k (h t d) -> p k h t d", h=4, t=2)
                        for tt_ in range(4):
                            acc = accs.next()
                            for k in range(4):
                                b.mm(acc[:].rearrange("p (h d) -> p h d", h=4), cqn[:, k, tt_ * 128:(tt_ + 1) * 128],
                                     wv4[:, k, :, 1, :], k == 0, k == 3, [wukv.tile, cqn.tile], [acc.tile])
                            ob = obf.next()
                            b.copy("dve", ob[:], acc[:], [acc.tile], [ob.tile])
                            r0 = sbk * 512 + tt_ * 128
                            b.dma("pool", io["vsrc"][sbk][tt_ * 128:(tt_ + 1) * 128, VA0:VA0 + 512], ob[:], [ob.tile], [otile("v", sbk, "a", tt_)])
                wb = load_w(1024, 64)
                acc = accs.next()
                fmm(acc, wb, 0, 64, sbk)
                sbf = stg_b.next()
                b.copy("dve", sbf[0:64, :], acc[0:64, :], [acc.tile], [sbf.tile])
                rope_store(acc[0:64, :], acc.tile, sbf, 64, 0, 0, io["ksrc"][1][0:64, tsl], otile("k", 1, 0, sbk), cst)
                pendB = None
                for g in range(3):
                    wb = load_w(1088 + g * 512, 512)
                    for c in range(4):
                        ch = g * 4 + c
                        acc = accs.next()
                        fmm(acc, wb, c * 128, 128, sbk)
                        sbf = stg_b.next()
                        b.copy("dve", sbf[:], acc[:], [acc.tile], [sbf.tile])
                        if ch < 6:
                            dst = qt_o[QB0 + ch * 128:QB0 + (ch + 1) * 128, tsl]
                            dtl = otile("q", "b", ch, sbk)
                        else:
                            ci_, lr_ = kloc(KB0 + (ch - 6) * 128)
                            dst = io["ksrc"][ci_][lr_:lr_ + 128, tsl]
                            dtl = otile("k", ci_, lr_, sbk)
                        if pendB is not None:
                            rope_store(*pendB)
                        pendB = (acc[:], acc.tile, sbf, 128, 1, 1, dst, dtl, cst)
                rope_store(*pendB)
                for (c0, ncol, vcol) in ((2624, 512, VB0), (3136, 256, VB0 + 512), (4416, 256, VC0)):
                    wb = load_w(c0, ncol)
                    for tt_ in range(4):
                        acc = accs.next()
                        for k in range(16):
                            b.mm(acc[:, 0:ncol], hTs[sbk][:, k, tt_ * 128:(tt_ + 1) * 128],
                                 wb[:, k, 0:ncol], k == 0, k == 15, [wb.tile, hTs[sbk].tile], [acc.tile])
                        ob = obf.next()
                        b.copy("dve", ob[:, 0:ncol], acc[:, 0:ncol], [acc.tile], [ob.tile])
                        r0 = sbk * 512 + tt_ * 128
                        b.dma("pool", io["vsrc"][sbk][tt_ * 128:(tt_ + 1) * 128, vcol:vcol + ncol], ob[:, 0:ncol], [ob.tile], [otile("v", sbk, vcol, tt_)])
                def c_stage_b(ch, qf, sq):
                    sp_ = aux.next()
                    b.mm(sp_[:], ones[:], sq[:], True, True, [ones.tile, sq.tile], [sp_.tile])
                    rs = rsb.next()
                    b.copy("dve", rs[:], sp_[:], [sp_.tile], [rs.tile])
                    b.rstd(rs, rs[:], 128)
                    gcol = 8 if ch < 6 else 9
                    b.stt("dve", qf[:], qf[:], gains[:, gcol:gcol + 1], rs[:], ALU.mult, ALU.mult,
                          [qf.tile, gains.tile, rs.tile], [qf.tile])
                    sbf = stg_b.next()
                    b.copy("dve", sbf[:], qf[:], [qf.tile], [sbf.tile])
                    if ch < 6:
                        dst = qt_o[QC0 + ch * 128:QC0 + (ch + 1) * 128, tsl]
                        dtl = otile("q", "c", ch, sbk)
                    else:
                        ci_, lr_ = kloc(KC0 + (ch - 6) * 128)
                        dst = io["ksrc"][ci_][lr_:lr_ + 128, tsl]
                        dtl = otile("k", ci_, lr_, sbk)
                    return (qf[:], qf.tile, sbf, 128, 2, 2, dst, dtl, cst)

                pendA, pendR = None, None
                for g in range(2):
                    wb = load_w(3392 + g * 512, 512)
                    for c in range(4):
                        ch = g * 4 + c
                        acc = accs.next()
                        fmm(acc, wb, c * 128, 128, sbk)
                        qf = stg_f.next()
                        b.copy("dve", qf[:], acc[:], [acc.tile], [qf.tile])
                        sq = sqb.next()
                        b.act(sq[:], qf[:], AF.Square, [qf.tile], [sq.tile])
                        newR = c_stage_b(*pendA) if pendA is not None else None
                        if pendR is not None:
                            rope_store(*pendR)
                        pendR = newR
                        pendA = (ch, qf, sq)
                newR = c_stage_b(*pendA)
                if pendR is not None:
                    rope_store(*pendR)
                rope_store(*newR)
                if sbk < 3:
                    norm_transpose(x_src, (sbk + 1) * 512, 4, xdeps, io["attn_norm"], hTs[sbk + 1], tmp)
            b.pop()

        def cast_ffn_weights():
            LL = io["L"]
            for nm, src, dst, rows in (("g", io["w_gate"], io["wgb"], D), ("u", io["w_up"], io["wub"], D), ("d", io["w_down"], io["wdb"], DFF)):
                h = rows // 2
                for part in range(2):
                    b.dma("pool", dst[part * h:(part + 1) * h, :], src[part * h:(part + 1) * h, :], [], [dtile("wc", LL, nm, part)])

        def exchange():
            LL = io["L"]
            grp = [[0, 1], [2, 3], [4, 5], [6, 7]]

            def cc(src, dst, rtiles, wtile):
                o = P.op("pool", lambda e: e.collective_compute("AllGather", ALU.bypass, replica_groups=grp,
                                                                ins=[src.ap().opt()], outs=[dst.ap().opt()]),
                         reads=rtiles, writes=[wtile], dma=True)
                o.cc = True

            for i_ in range(len(KCH)):
                rt = [t for k_, t in dts.items() if k_[0] == "o" and k_[1] == LL and k_[2] == "k" and k_[3] == i_]
                assert rt
                cc(io["ksrc"][i_], io["kdst"][i_], rt, dtile("agk", LL, i_))
            for j_ in range(4):
                rt = [t for k_, t in dts.items() if k_[0] == "o" and k_[1] == LL and k_[2] == "v" and k_[3] == j_]
                assert rt
                cc(io["vsrc"][j_], io["vdst"][j_], rt, dtile("agv", LL, j_))

        def attention(mixers="ABC"):
            qt = io["qt"]
            LL = io["L"]
            qtiles = [t for k_, t in dts.items() if k_[0] == "o" and k_[1] == LL and k_[2] == "q"]
            vtiles = [dtile("agv", LL, j_) for j_ in range(4)]

            def kload(dst_buf, npart, row):
                ci_, lr = kloc(row)
                nrows = KCH[ci_][1]
                kd = io["kdst"][ci_]
                P.op("sp", lambda e: e.dma_start(out=dst_buf[0:npart, 0:TOK],
                                                 in_=kd.ap().rearrange("(a p) c -> a p c", a=2)[bass.ds(own_of(e), 1), lr:lr + npart, :].rearrange("a p c -> (a p) c")),
                     reads=[dtile("agk", LL, ci_)], writes=[dst_buf.tile], dma=True)
                P.op("sp", lambda e: e.dma_start(out=dst_buf[0:npart, TOK:2 * TOK],
                                                 in_=kd.ap().rearrange("(a p) c -> a p c", a=2)[bass.ds(par_of(e), 1), lr:lr + npart, :].rearrange("a p c -> (a p) c")),
                     reads=[dtile("agk", LL, ci_)], writes=[dst_buf.tile], dma=True)

            def vload(vbuf_, j_, vc0_, vw_):
                vd = io["vdst"][j_]
                P.op("sp", lambda e: e.dma_start(out=vbuf_[:, j_ * 4:(j_ + 1) * 4, 0:vw_],
                                                 in_=vd.ap().rearrange("(a r) c -> a r c", a=2)[bass.ds(own_of(e), 1), :, vc0_:vc0_ + vw_].rearrange("a (t p) c -> p (a t) c", p=128)),
                     reads=[vtiles[j_]], writes=[vbuf_.tile], dma=True)
                P.op("sp", lambda e: e.dma_start(out=vbuf_[:, 16 + j_ * 4:16 + (j_ + 1) * 4, 0:vw_],
                                                 in_=vd.ap().rearrange("(a r) c -> a r c", a=2)[bass.ds(par_of(e), 1), :, vc0_:vc0_ + vw_].rearrange("a (t p) c -> p (a t) c", p=128)),
                     reads=[vtiles[j_]], writes=[vbuf_.tile], dma=True)

            b.push()
            masks = b.sb([128, NMASK, 512], BF16, "masks")
            for m0 in range(0, NMASK, 4):
                b.dma("sp", masks[:, m0:m0 + 4, :], bmask_d[m0:m0 + 4].rearrange("m p t -> p m t"), [], [masks.tile])
            vbuf = b.sb([128, 32, 768], BF16, "vbuf")
            kTs = Rot([b.sb([128, S], BF16) for _ in range(2)])
            qTs = Rot([b.sb([128, TOK], BF16) for _ in range(2)])
            kr = b.sb([128, S], BF16, "kr")
            qrs = Rot([b.sb([128, TOK], BF16) for _ in range(2)])
            pTs = Rot([b.sb([128, 512], BF16) for _ in range(5)])
            rdens = Rot([b.sb([128, 512], F32) for _ in range(2)])
            ys = Rot([b.sb([128, 512], F32) for _ in range(2)])
            sT = Rot([pb[0], pb[1], pb[2], pb[7]])
            oT = Rot([pb[3], pb[4]])
            dn = Rot([pb[5], pb[6]])
            specs = []
            if "A" in mixers:
                specs.append(("A", 4, VA0, 512, 192.0 ** -0.5))
            if "B" in mixers:
                specs.append(("B", 6, VB0, 768, 128.0 ** -0.5))
            if "C" in mixers:
                specs.append(("C", 6, VC0, 256, 128.0 ** -0.5))
            for (mx, nh, vc0, vw, scale) in specs:
                for j_ in range(4):
                    vload(vbuf, j_, vc0, vw)
                if mx == "A":
                    kload(kr, 64, KAR0)
                kT = None
                for h in range(nh):
                    if mx == "A":
                        krow, qrow, vcol, yrow = KA0 + h * 128, QA0 + h * 192, h * 128, h * 128
                    elif mx == "B":
                        krow, qrow, vcol, yrow = KB0 + h * 128, QB0 + h * 128, h * 128, 512 + h * 128
                    else:
                        krow, qrow, vcol, yrow = KC0 + (h // 3) * 128, QC0 + h * 128, (h // 3) * 128, 1280 + h * 128
                    if not (mx == "C" and h % 3 != 0):
                        kT = kTs.next()
                        kload(kT, 128, krow)
                    qT = qTs.next()
                    b.dma("sp", qT[:, :], qt[qrow:qrow + 128, :], qtiles, [qT.tile])
                    if mx == "A":
                        qr = qrs.next()
                        b.dma("sp", qr[0:64, :], qt[qrow + 128:qrow + 192, :], qtiles, [qr.tile])
                    for qb in range(4):
                        qsl = slice(qb * 512, (qb + 1) * 512)
                        if mx == "B":
                            tiles = []
                            q0 = qb * 512
                            for delta in range(-1024, 1409, 128):
                                k0 = q0 + delta
                                if k0 < 0:
                                    tiles.append((S + k0, 20 + (delta + 1024) // 128))
                                elif k0 < TOK:
                                    tiles.append((k0, (delta + 1024) // 128))
                                else:
                                    tiles.append((k0, 28 + (delta - 512) // 128))
                        else:
                            tiles = [(t * 128, None) for t in range(32)]
                        pump(4)
                        o_ps = oT.next()
                        d_ps = dn.next()
                        n = len(tiles)
                        def pv(i_, col_, pT_):
                            b.mm(o_ps[:], vbuf[:, col_ // 128, vcol:vcol + 128], pT_[:], i_ == 0, i_ == n - 1,
                                 [vbuf.tile, pT_.tile], [o_ps.tile])
                            b.mm(d_ps[:], ones[:], pT_[:], i_ == 0, i_ == n - 1, [ones.tile, pT_.tile], [d_ps.tile])

                        pend = []
                        for i, (col, mi) in enumerate(tiles):
                            s_ps = sT.next()
                            if mx == "A":
                                b.mm(s_ps[:], kT[:, col:col + 128], qT[:, qsl], True, False, [kT.tile, qT.tile], [s_ps.tile])
                                b.mm(s_ps[:], kr[0:64, col:col + 128], qr[0:64, qsl], False, True, [kr.tile, qr.tile], [s_ps.tile])
                            else:
                                b.mm(s_ps[:], kT[:, col:col + 128], qT[:, qsl], True, True, [kT.tile, qT.tile], [s_ps.tile])
                            pT = pTs.next()
                            b.act(pT[:], s_ps[:], AF.Exp, [s_ps.tile], [pT.tile], scale=float(scale))
                            if mi is not None:
                                b.tt("dve", pT[:], pT[:], masks[:, mi, :], ALU.mult, [pT.tile, masks.tile], [pT.tile])
                            pend.append((i, col, pT))
                            if len(pend) > 2:
                                pv(*pend.pop(0))
                        while pend:
                            pv(*pend.pop(0))
                        rd = rdens.next()
                        b.recip(rd[:], d_ps[:], [d_ps.tile], [rd.tile])
                        y = ys.next()
                        b.tt("dve", y[:], o_ps[:], rd[:], ALU.mult, [o_ps.tile, rd.tile], [y.tile])
                        b.dma("pool", yT[yrow:yrow + 128, qsl], y[:], [y.tile], [dtile("yT", yrow // 128, qb)])
            b.pop()

        def phase3(x_res, xdeps):
            b.push()
            wo = b.sb([128, 16, D], BF16, "wo")
            drain(io["L"], "out")
            for k in range(16):
                b.dma("sp", wo[:, k, :], io["woutb"][k * 128:(k + 1) * 128, :], wtiles[(io["L"], "out")], [wo.tile])
            og = b.sb([128, 16], F32, "og")
            b.dma("sp", og[:], io["out_norm"].rearrange("(k p) -> p k", p=128), [], [og.tile], slow=True)
            ybs = Rot([b.sb([128, 16, 512], F32) for _ in range(1)])
            yn = b.sb([128, 16, 512], BF16, "yn")
            sqb = Rot([b.sb([128, 512], BF16) for _ in range(3)])
            rsb = Rot([b.sb([128, 512], F32) for _ in range(2)])
            xts = Rot([b.sb([128, D], F32) for _ in range(2)])
            accs = Rot([pb[0], pb[1], pb[2], pb[3]])
            aux = Rot([pb[4], pb[5]])
            groups = ((0, 4), (4, 10), (10, 16))
            for tb in range(4):
                tsl = slice(tb * 512, (tb + 1) * 512)
                yb = ybs.next()
                for c in range(16):
                    b.dma("sp", yb[:, c, :], yT[c * 128:(c + 1) * 128, tsl], [dtile("yT", c, tb)], [yb.tile])
                for (g0, g1) in groups:
                    sp_ = aux.next()
                    for c in range(g0, g1):
                        sq = sqb.next()
                        b.act(sq[:], yb[:, c, :], AF.Square, [yb.tile], [sq.tile])
                        b.mm(sp_[:], ones[:], sq[:], c == g0, c == g1 - 1, [ones.tile, sq.tile], [sp_.tile])
                    rs = rsb.next()
                    b.copy("dve", rs[:], sp_[:], [sp_.tile], [rs.tile])
                    b.rstd(rs, rs[:], (g1 - g0) * 128)
                    for c in range(g0, g1):
                        b.stt("dve", yn[:, c, :], yb[:, c, :], og[:, c:c + 1], rs[:], ALU.mult, ALU.mult,
                              [yb.tile, og.tile, rs.tile], [yn.tile])
                for tt_ in range(4):
                    xt = xts.next()
                    r0 = tb * 512 + tt_ * 128
                    b.dma("sp", xt[:], x_res[r0:r0 + 128, :], [xdeps(r0 // 128)] if xdeps else [], [xt.tile])
                    for cg in range(4):
                        acc = accs.next()
                        for k in range(16):
                            b.mm(acc[:], yn[:, k, tt_ * 128:(tt_ + 1) * 128], wo[:, k, cg * 512:(cg + 1) * 512],
                                 k == 0, k == 15, [yn.tile, wo.tile], [acc.tile])
                        b.tt("dve", xt[:, cg * 512:(cg + 1) * 512], acc[:], xt[:, cg * 512:(cg + 1) * 512], ALU.add,
                             [acc.tile, xt.tile], [xt.tile])
                    b.dma("pool", x1s[r0:r0 + 128, :], xt[:], [xt.tile], [dtile("x1", r0 // 128)])
            b.pop()

        def ffn(do_final, x_dst, dkey):
            b.push()
            FG = 256
            h2T = b.sb([128, 16, 512], BF16, "h2T")
            x1k = [b.sb([128, D], F32) for _ in range(4)]
            aT = b.sb([128, 44, 512], BF16, "aT")
            wgs = Rot([b.sb([128, 16, FG], BF16) for _ in range(2)])
            wus = Rot([b.sb([128, 16, FG], BF16) for _ in range(2)])
            wds = Rot([b.sb([128, 44, 256], BF16) for _ in range(2)])
            tmp = dict(gb=b.sb([128, D], F32), xts=None, xn=Rot([b.sb([128, D], BF16) for _ in range(1)]),
                       sss=Rot([b.sb([128, 1], F32) for _ in range(2)]), tps=Rot([pb[6], pb[7]]))
            sgs = Rot([b.sb([128, 512], F32) for _ in range(2)])
            gps = Rot([pb[0], pb[1]])
            ups = Rot([pb[2], pb[3]])
            dps = Rot([pb[4], pb[5]])
            LL = io["L"]
            wgb, wub, wdb = io["wgb"], io["wub"], io["wdb"]
            drain(LL, "d")
            cg_t, cu_t, cd_t = wtiles[(LL, "g")], wtiles[(LL, "u")], wtiles[(LL, "d")]
            for tb in range(4):
                norm_transpose(x1s, tb * 512, 4, lambda i: dtile("x1", i), io["ffn_norm"], h2T, tmp, keep=x1k)
                for fg in range(DFF // FG):
                    pump(1)
                    wg = wgs.next()
                    wu = wus.next()
                    b.dma("sp", wg[:, :, :], wgb[fg], cg_t[fg * 4:(fg + 1) * 4], [wg.tile])
                    b.dma("sp", wu[:, :, :], wub[fg], cu_t[fg * 4:(fg + 1) * 4], [wu.tile])
                    for c in range(FG // 128):
                        f = fg * (FG // 128) + c
                        g_ps = gps.next()
                        u_ps = ups.next()
                        for k in range(16):
                            b.mm(g_ps[:], wg[:, k, c * 128:(c + 1) * 128], h2T[:, k, :], k == 0, k == 15,
                                 [wg.tile, h2T.tile], [g_ps.tile])
                        for k in range(16):
                            b.mm(u_ps[:], wu[:, k, c * 128:(c + 1) * 128], h2T[:, k, :], k == 0, k == 15,
                                 [wu.tile, h2T.tile], [u_ps.tile])
                        sg = sgs.next()
                        b.act(sg[:], g_ps[:], AF.Silu, [g_ps.tile], [sg.tile])
                        b.tt("dve", aT[:, f, :], u_ps[:], sg[:], ALU.mult, [u_ps.tile, sg.tile], [aT.tile])
                for cg in range(D // 256):
                    pump(1)
                    wd = wds.next()
                    for f0 in (0, 22):
                        b.dma("sp", wd[:, f0:f0 + 22, :], wdb[cg, :, f0:f0 + 22, :], cd_t[cg * 11:(cg + 1) * 11], [wd.tile])
                    for tt_ in range(4):
                        acc = dps.next()
                        for f in range(44):
                            b.mm(acc[:, 0:256], aT[:, f, tt_ * 128:(tt_ + 1) * 128], wd[:, f, :], f == 0, f == 43,
                                 [aT.tile, wd.tile], [acc.tile])
                        xk = x1k[tt_]
                        b.tt("dve", xk[:, cg * 256:(cg + 1) * 256], acc[:, 0:256], xk[:, cg * 256:(cg + 1) * 256], ALU.add,
                             [acc.tile, xk.tile], [xk.tile])
                for tt_ in range(4):
                    xk = x1k[tt_]
                    r0 = tb * 512 + tt_ * 128
                    if do_final:
                        gb = tmp["gb"]
                        if tt_ == 0:
                            b.dma("sp", gb[:], final_norm_d.partition_broadcast(128), [], [gb.tile])
                        xn = tmp["xn"].next()
                        ss = tmp["sss"].next()
                        b.act(xn[:], xk[:], AF.Square, [xk.tile], [xn.tile, ss.tile], accum_out=ss[:, 0:1])
                        b.rstd(ss, ss[:, 0:1], D)
                        b.stt("dve", xk[:], xk[:], ss[:, 0:1], gb[:], ALU.mult, ALU.mult, [xk.tile, ss.tile, gb.tile], [xk.tile])
                    t_ = dtile(dkey, r0 // 128)
                    b.dma("pool", x_dst[r0:r0 + 128, :], xk[:], [xk.tile], [t_])
                    if do_final:
                        out_tiles.append(t_)
            b.pop()

        for l in range(2):
            io = IO[l]
            if l == 0:
                phase1(x_ext, None)
            else:
                phase1(x2s, lambda i: dtile("x2", i))
            exchange()
            attention()
            if l == 0:
                phase3(x_ext, None)
                ffn(False, x2s, "x2")
            else:
                phase3(x2s, lambda i: dtile("x2", i))
                ffn(True, x_out, "xo")

        P.op("sp", lambda e: e.nop(), reads=out_tiles)
        P.op("pool", lambda e: e.nop(), reads=out_tiles)
        b.P.finalize(st)
    return nc


def _rope_tables():
    f32 = np.float32
    pos = np.arange(S, dtype=np.int32)
    row = pos // 64
    col = pos % 64

    def ang(p, dim):
        inv = (f32(10000.0) ** (-np.arange(0, dim, 2, dtype=f32) / f32(dim))).astype(f32)
        return p.astype(f32)[:, None] * inv[None, :]

    out = np.zeros((6, 128, S), f32)
    aa = ang(pos, 64)
    ab = ang(pos, 128)
    ar = ang(row, 64)
    ac = ang(col, 64)
    A = np.concatenate([aa, aa], 1)
    Bm = np.concatenate([ab, ab], 1)
    C = np.concatenate([ar, ar, ac, ac], 1)
    out[0, :64] = np.cos(A).T
    out[1, :64] = np.sin(A).T
    out[2] = np.cos(Bm).T
    out[3] = np.sin(Bm).T
    out[4] = np.cos(C).T
    out[5] = np.sin(C).T
    return out


def _rot_mats():
    r = np.zeros((3, 128, 128), np.float32)

    def fill(m, off, n):
        h = n // 2
        for j in range(h):
            m[off + j + h, off + j] = -1.0
        for j in range(h, n):
            m[off + j - h, off + j] = 1.0

    fill(r[0], 0, 64)
    fill(r[1], 0, 128)
    fill(r[2], 0, 64)
    fill(r[2], 64, 64)
    return r


def _bmasks(hf):
    i = np.arange(128)[:, None]
    j = np.arange(512)[None, :]
    own = np.zeros((20, 128, 512), np.float32)
    for m in range(20):
        r = (-1024 + 128 * m) + i - j
        a = np.abs(r)
        own[m] = (a <= 64).astype(np.float32) + ((a <= 256) & (r % 4 == 0)) + ((a <= 1024) & (r % 16 == 0))
    out = np.zeros((NMASK, 128, 512), np.float32)
    out[0:20] = own
    if hf == 1:
        out[20:28] = own[0:8]
    else:
        out[28:36] = own[12:20]
    return out


_CACHE = {}


def _prog():
    if "F" not in _CACHE:
        _CACHE["F"] = build_program()
    return _CACHE["F"]


_W_KEYS = ("attn_norm", "w_in", "a_q_norm", "a_kv_norm", "a_w_uq", "a_w_ukv", "c_q_norm", "c_k_norm",
           "out_norm", "w_out", "ffn_norm", "w_gate", "w_up", "w_down")


def kernel(**inputs):
    inp = {k: np.asarray(v) for k, v in inputs.items()}
    x = inp["x"]
    cs = _rope_tables()
    rots = _rot_mats()
    cores = list(range(8))
    bms = [_bmasks(0).astype(ml_dtypes.bfloat16), _bmasks(1).astype(ml_dtypes.bfloat16)]
    wts = {}
    for l in range(2):
        for k in _W_KEYS:
            wts["%s%d" % (k, l)] = np.ascontiguousarray(inp[k][l])
    fn = np.ascontiguousarray(inp["final_norm"])
    maps = []
    for c in cores:
        hf = c % 2
        m = dict(x=np.ascontiguousarray(x[c // 2, hf * TOK:(hf + 1) * TOK]), rots=rots,
                 cs=np.ascontiguousarray(cs[:, :, hf * TOK:(hf + 1) * TOK]), bmask=bms[hf], final_norm=fn)
        m.update(wts)
        maps.append(m)
    res = run_bass_kernel_spmd(_prog(), maps, core_ids=cores).results
    out = np.empty((4, S, D), np.float32)
    for c in cores:
        out[c // 2, (c % 2) * TOK:(c % 2 + 1) * TOK] = res[c]["x_out"]
    return out
```
